# Optimizing a Trainium2 kernel written in Bass

```python
import jax, jax.numpy as jnp
from jax import lax
import numpy as np

D_MODEL = 1024
BATCH = 16
SEQ = 4096
DEPTH = 4
DEC_BATCH = 32
DEC_SEQ = 2048
PAST_LEN = 128

GRID_W = 64
HEAD_DIM = 64
NA_HEADS = 8
NA_WIDTH = NA_HEADS * HEAD_DIM
NA_WIN_H_MAX = 8
NA_WIN_W = 16
GQA_Q_HEADS = 8
GQA_KV_HEADS = 2
GQA_Q_WIDTH = GQA_Q_HEADS * HEAD_DIM
GQA_KV_WIDTH = GQA_KV_HEADS * HEAD_DIM
ROPE_AXIS_DIM = HEAD_DIM // 2
ROPE_THETA = 10000.0
Q_BLOCK = 128
EPS = 1e-6
IN_SPLITS = (NA_WIDTH, NA_WIDTH, NA_WIDTH, NA_WIDTH,
             GQA_Q_WIDTH, GQA_KV_WIDTH, GQA_KV_WIDTH, GQA_Q_WIDTH,
             D_MODEL, D_MODEL)
D_IN = sum(IN_SPLITS)

kernel_name = "hybrid_natten_gqa_axial_encoder"


def _rms_norm(x, g):
    xf = x.astype(jnp.float32)
    y = xf * lax.rsqrt(jnp.mean(xf * xf, axis=-1, keepdims=True) + EPS)
    return (y * g.astype(jnp.float32)).astype(x.dtype)


def _axial_angles(n):
    t = jnp.arange(n)
    row = (t // GRID_W).astype(jnp.float32)
    col = (t % GRID_W).astype(jnp.float32)
    inv = ROPE_THETA ** (-jnp.arange(0, ROPE_AXIS_DIM, 2, dtype=jnp.float32) / ROPE_AXIS_DIM)
    return row[:, None] * inv, col[:, None] * inv


def _rotate_half(xf, ang):
    h = xf.shape[-1] // 2
    x1, x2 = xf[..., :h], xf[..., h:]
    cos = jnp.cos(ang)[None, :, None, :]
    sin = jnp.sin(ang)[None, :, None, :]
    return jnp.concatenate([x1 * cos - x2 * sin, x1 * sin + x2 * cos], axis=-1)


def _axial_rope(x):
    n = x.shape[1]
    ang_r, ang_c = _axial_angles(n)
    xf = x.astype(jnp.float32)
    out = jnp.concatenate([_rotate_half(xf[..., :ROPE_AXIS_DIM], ang_r),
                           _rotate_half(xf[..., ROPE_AXIS_DIM:], ang_c)], axis=-1)
    return out.astype(x.dtype)


def _neighbourhood_attention(q, k, v, rpb):
    b, n, h, dh = q.shape
    rows = n // GRID_W
    kh = min(NA_WIN_H_MAX, rows)
    kw = NA_WIN_W
    q = q.reshape(b, rows, GRID_W, h, dh)
    k = k.reshape(b, rows, GRID_W, h, dh)
    v = v.reshape(b, rows, GRID_W, h, dh)
    cols = np.arange(GRID_W)
    col_start = np.clip(cols - kw // 2, 0, GRID_W - kw)
    col_idx = col_start[:, None] + np.arange(kw)[None, :]
    col_off = col_idx - cols[:, None] + (kw - 1)
    rpb_cols = rpb[:, :, col_off].astype(jnp.float32)
    scale = HEAD_DIM ** -0.5

    def row_block(r):
        r0 = jnp.clip(r - kh // 2, 0, rows - kh)
        q_r = lax.dynamic_index_in_dim(q, r, axis=1, keepdims=False)
        k_band = lax.dynamic_slice_in_dim(k, r0, kh, axis=1)
        v_band = lax.dynamic_slice_in_dim(v, r0, kh, axis=1)
        k_g = k_band[:, :, col_idx]
        v_g = v_band[:, :, col_idx]
        s = jnp.einsum('bqhd,baqchd->bhqac', q_r, k_g).astype(jnp.float32) * scale
        row_off = r0 + jnp.arange(kh) - r + (NA_WIN_H_MAX - 1)
        bias = jnp.take(rpb_cols, row_off, axis=1)
        s = s + jnp.transpose(bias, (0, 2, 1, 3))[None]
        p = jax.nn.softmax(s.reshape(b, h, GRID_W, kh * kw), axis=-1).reshape(s.shape)
        return jnp.einsum('bhqac,baqchd->bqhd', p.astype(v.dtype), v_g)

    out = lax.map(row_block, jnp.arange(rows))
    return jnp.transpose(out, (1, 0, 2, 3, 4)).reshape(b, n, h * dh)


def _gqa_attention(q, k, v):
    b, n, hq, dh = q.shape
    hkv = k.shape[2]
    g = hq // hkv
    nb = n // Q_BLOCK
    qb = jnp.transpose(q.reshape(b, nb, Q_BLOCK, hkv, g, dh), (1, 0, 2, 3, 4, 5))
    scale = HEAD_DIM ** -0.5

    def block(qblk):
        s = jnp.einsum('bqkgd,bnkd->bkgqn', qblk, k).astype(jnp.float32) * scale
        p = jax.nn.softmax(s, axis=-1)
        return jnp.einsum('bkgqn,bnkd->bqkgd', p.astype(v.dtype), v)

    out = lax.map(block, qb)
    return jnp.transpose(out, (1, 0, 2, 3, 4, 5)).reshape(b, n, hq * dh)


def _layer(x, c, norm_g, w_ada, b_ada, w_in, b_gate, rpb, q_norm_g, k_norm_g, w_pa, w_pb, w_out):
    b, n, _ = x.shape
    mod = jax.nn.silu(c) @ w_ada + b_ada
    shift, scale, gate = jnp.split(mod, 3, axis=-1)
    hid = _rms_norm(x, norm_g) * (1.0 + scale[:, None, :]) + shift[:, None, :]
    proj = hid @ w_in
    split_at = np.cumsum(IN_SPLITS)[:-1].tolist()
    qa, ka, va, za, qb, kb, vb, zb, ga, gb = jnp.split(proj, split_at, axis=-1)
    oa = _neighbourhood_attention(qa.reshape(b, n, NA_HEADS, HEAD_DIM),
                                  ka.reshape(b, n, NA_HEADS, HEAD_DIM),
                                  va.reshape(b, n, NA_HEADS, HEAD_DIM), rpb)
    ya = (oa * jax.nn.silu(za)) @ w_pa
    qh = _axial_rope(_rms_norm(qb.reshape(b, n, GQA_Q_HEADS, HEAD_DIM), q_norm_g))
    kh = _axial_rope(_rms_norm(kb.reshape(b, n, GQA_KV_HEADS, HEAD_DIM), k_norm_g))
    ob = _gqa_attention(qh, kh, vb.reshape(b, n, GQA_KV_HEADS, HEAD_DIM))
    yb = (ob * jax.nn.silu(zb)) @ w_pb
    g_all = jnp.concatenate([ga, gb], axis=-1) + b_gate
    g_a, g_b = jnp.split(jax.nn.sigmoid(g_all), 2, axis=-1)
    merged = g_a * ya + g_b * yb
    return x + gate[:, None, :] * (merged @ w_out)


def _trunk(x, c, norm_g, w_ada, b_ada, w_in, b_gate, rpb, q_norm_g, k_norm_g, w_pa, w_pb, w_out, final_g):
    for l in range(DEPTH):
        x = _layer(x, c, norm_g[l], w_ada[l], b_ada[l], w_in[l], b_gate[l], rpb[l],
                   q_norm_g[l], k_norm_g[l], w_pa[l], w_pb[l], w_out[l])
    return _rms_norm(x, final_g)


def setup_inputs(seed: int = 0) -> dict:
    key = jax.random.key(seed)
    ks = jax.random.split(key, 16)
    f32 = jnp.float32
    d = D_MODEL
    nrm = lambda k, shape: jax.random.normal(k, shape, dtype=f32)
    return {
        "x_prompt": nrm(ks[0], (BATCH, SEQ, d)),
        "x_sample": nrm(ks[1], (DEC_BATCH, DEC_SEQ, d)),
        "c_prompt": nrm(ks[2], (BATCH, d)),
        "c_sample": nrm(ks[3], (DEC_BATCH, d)),
        "norm_g": 1.0 + 0.02 * nrm(ks[4], (DEPTH, d)),
        "w_ada": nrm(ks[5], (DEPTH, d, 3 * d)) * (0.5 * d ** -0.5),
        "b_ada": 0.01 * nrm(ks[6], (DEPTH, 3 * d)),
        "w_in": nrm(ks[7], (DEPTH, d, D_IN)) * d ** -0.5,
        "b_gate": 0.01 * nrm(ks[8], (DEPTH, 2 * d)),
        "rpb": 0.02 * nrm(ks[9], (DEPTH, NA_HEADS, 2 * NA_WIN_H_MAX - 1, 2 * NA_WIN_W - 1)),
        "q_norm_g": 1.0 + 0.02 * nrm(ks[10], (DEPTH, HEAD_DIM)),
        "k_norm_g": 1.0 + 0.02 * nrm(ks[11], (DEPTH, HEAD_DIM)),
        "w_pa": nrm(ks[12], (DEPTH, NA_WIDTH, d)) * NA_WIDTH ** -0.5,
        "w_pb": nrm(ks[13], (DEPTH, GQA_Q_WIDTH, d)) * GQA_Q_WIDTH ** -0.5,
        "w_out": nrm(ks[14], (DEPTH, d, d)) * d ** -0.5,
        "final_g": 1.0 + 0.02 * nrm(ks[15], (d,)),
    }


def reference(x_prompt, x_sample, c_prompt, c_sample, norm_g, w_ada, b_ada, w_in, b_gate, rpb,
              q_norm_g, k_norm_g, w_pa, w_pb, w_out, final_g):
    y_prompt = _trunk(x_prompt, c_prompt, norm_g, w_ada, b_ada, w_in, b_gate, rpb,
                      q_norm_g, k_norm_g, w_pa, w_pb, w_out, final_g)
    y_sample = _trunk(x_sample, c_sample, norm_g, w_ada, b_ada, w_in, b_gate, rpb,
                      q_norm_g, k_norm_g, w_pa, w_pb, w_out, final_g)
    return (y_prompt, y_sample)
```

```python
import numpy as np
from contextlib import ExitStack
import concourse.bass as bass
import concourse.mybir as mybir
from concourse.bass_utils import run_bass_kernel_spmd

F32 = mybir.dt.float32
BF16 = mybir.dt.bfloat16
AF = mybir.ActivationFunctionType
ALU = mybir.AluOpType
AX = mybir.AxisListType

D = 1024
DIN = 5376
QA, KA, VA, ZA, QB, KB, VB_, ZB, GA, GB = 0, 512, 1024, 1536, 2048, 2560, 2688, 2816, 3328, 4352
EPS = 1e-6
NEG = -30000.0
RING = 5


class Buf:
    __slots__ = ("name", "w", "r")

    def __init__(self, name):
        self.name = name
        self.w = None
        self.r = {}


class Eng:
    def __init__(self, name, h, sem, in_order):
        self.name, self.h, self.sem, self.in_order = name, h, sem, in_order
        self.cnt = 0
        self.waited = {}


class FW:
    def __init__(self, nc, es):
        self.nc, self.es = nc, es
        mk = lambda n: es.enter_context(nc.semaphore(n))
        self.pe = Eng("pe", nc.tensor, mk("s_pe"), True)
        self.act = Eng("act", nc.scalar, mk("s_act"), False)
        self.dve = Eng("dve", nc.vector, mk("s_dve"), False)
        self.pool = Eng("pool", nc.gpsimd, mk("s_pool"), False)
        self.sp = Eng("sp", nc.sync, mk("s_sp"), False)
        self.engs = [self.pe, self.act, self.dve, self.pool, self.sp]
        self.dma_sems = {}
        self.ninst = 0
        self.q = None

    def begin_defer(self):
        self.q = []

    def end_defer(self):
        q, self.q = self.q, None
        return q

    def _deps(self, eng, reads, writes, is_dma):
        need = {}

        def add(tok, raw):
            sem, val, src = tok
            if src is eng and not is_dma:
                if eng.in_order:
                    return
            if eng.waited.get(sem, 0) >= val:
                return
            if need.get(sem, 0) < val:
                need[sem] = val

        for b in reads:
            if b.w is not None:
                add(b.w, True)
        for b in writes:
            if b.w is not None:
                add(b.w, False)
            for t in b.r.values():
                add(t, False)
        return list(need.items())

    def _emit(self, eng, fns, need, inc_sem, inc_val):
        inline = need.pop() if need else None
        for s, v in need:
            eng.h.wait_ge(s, v)
            eng.waited[s] = v
        last = None
        first = True
        for fn in fns:
            ins = fn()
            self.ninst += 1
            if first and inline is not None:
                ins.wait_op(inline[0], inline[1], "sem-ge")
                eng.waited[inline[0]] = inline[1]
            first = False
            last = ins
        last.then_inc(inc_sem, inc_val)

    def _post(self, tok, reads, writes):
        for b in reads:
            b.r[tok[0]] = tok
        for b in writes:
            b.w = tok
            b.r = {}

    def op(self, eng, fns, reads=(), writes=()):
        if self.q is not None:
            self.q.append(lambda: self._op(eng, fns, reads, writes))
            return None
        return self._op(eng, fns, reads, writes)

    def _op(self, eng, fns, reads=(), writes=()):
        if not isinstance(fns, (list, tuple)):
            fns = [fns]
        need = self._deps(eng, reads, writes, False)
        self._emit(eng, fns, need, eng.sem, 1)
        eng.cnt += 1
        tok = (eng.sem, eng.cnt, eng)
        self._post(tok, reads, writes)
        return tok

    def dma(self, eng, fn, key, reads=(), writes=()):
        if self.q is not None:
            self.q.append(lambda: self._dma(eng, fn, key, reads, writes))
            return None
        return self._dma(eng, fn, key, reads, writes)

    def _dma(self, eng, fn, key, reads=(), writes=()):
        if key not in self.dma_sems:
            self.dma_sems[key] = [self.es.enter_context(self.nc.semaphore("d_" + key)), 0]
        ent = self.dma_sems[key]
        need = dict(self._deps(eng, reads, writes, True))
        if ent[1] > 0 and eng.waited.get(ent[0], 0) < ent[1]:
            need[ent[0]] = ent[1]
        self._emit(eng, [fn], list(need.items()), ent[0], 16)
        ent[1] += 16
        tok = (ent[0], ent[1], None)
        self._post(tok, reads, writes)
        return tok

    def finish(self):
        for e in self.engs:
            if e is not self.sp and e.cnt > 0:
                self.sp.h.wait_ge(e.sem, e.cnt)
        for k, ent in self.dma_sems.items():
            if ent[1] > 0:
                self.sp.h.wait_ge(ent[0], ent[1])


def na_plan(i, I):
    if i < 2:
        kts = [0, 1, 2, 3]
        return kts, [kt - i + 3 for kt in kts]
    if i >= I - 2:
        kts = [I - 4, I - 3, I - 2, I - 1]
        return kts, [kt - i + 3 for kt in kts]
    return [i - 2, i - 1, i, i + 1, i + 2], [7, 2, 3, 4, 8]


class _Stop(Exception):
    pass


STOP = None


def chk(n):
    if STOP is not None and n == STOP:
        raise _Stop()


def build_program(seq_lens, depth):
    try:
        return _build_program(seq_lens, depth)
    except _Stop as e:
        nc, fw = e.args
        return nc, fw


def _build_program(seq_lens, depth):
    NSEQ = len(seq_lens)
    TOT = sum(seq_lens)
    NMAX = max(seq_lens)
    nc = bass.Bass("TRN2", target_bir_lowering=False)
    es = ExitStack()
    fw = FW(nc, es)
    pe, act, dve, pool, sp = fw.pe, fw.act, fw.dve, fw.pool, fw.sp
    V, A, T, G = nc.vector, nc.scalar, nc.tensor, nc.gpsimd

    def dram(n, s, d=F32, k="ExternalInput"):
        return nc.dram_tensor(n, s, d, kind=k).ap()

    xin = dram("xin", [TOT, D])
    yout = dram("yout", [TOT, D], k="ExternalOutput")
    scr = dram("scr", [TOT, D], k="Internal")
    cT_d = dram("cT", [128, 8 * NSEQ])
    ngT_d = dram("ngT", [128, depth * 8])
    badaT_d = dram("badaT", [128, depth * 24])
    wada_d = dram("w_ada", [depth, D, 3 * D])
    win_d = dram("w_in", [depth, D, DIN])
    bgate_d = dram("b_gate", [depth, 2 * D])
    rpbg_d = dram("rpbg", [depth, 9, 128, 1024])
    qg_d = dram("q_norm_g", [depth, 64])
    kg_d = dram("k_norm_g", [depth, 64])
    wpa_d = dram("w_pa", [depth, 512, D])
    wpb_d = dram("w_pb", [depth, 512, D])
    wout_d = dram("w_out", [depth, D, D])
    fg_d = dram("final_g", [D])
    rope_d = dram("rope", [NMAX, 128])
    ident_d = dram("ident", [128, 128])

    def sb(n, s, d):
        return es.enter_context(nc.sbuf_tensor(n, s, d))

    wi = sb("wi", [128, 8, DIN], BF16)
    wpa = sb("wpa", [128, 4, D], BF16)
    wpb = sb("wpb", [128, 4, D], BF16)
    wo = sb("wo", [128, 8, D], BF16)
    E = sb("E", [128, 9, 1024], BF16)
    KBT = sb("KBT", [128, NMAX], BF16)
    VB = sb("VB", [128, NMAX // 128, 2, 66], BF16)
    KAT = sb("KAT", [128, 4, RING * 128], BF16)
    VAr = sb("VAr", [128, RING, 8, 66], BF16)
    hidT = sb("hidT", [128, 4, 8, 128], BF16)
    xr = sb("xr", [128, 2, D], F32)
    bg = sb("bg", [128, 2 * D], BF16)
    tokbf = sb("tokbf", [128, D], BF16)
    featbf = sb("featbf", [128, 8, 128], BF16)
    t1 = sb("t1", [128, 512], F32)
    t2 = sb("t2", [128, 512], F32)
    t3 = sb("t3", [128, 512], F32)
    qAT = sb("qAT", [128, 4, 128], BF16)
    qBT = sb("qBT", [128, 2, 4, 128], BF16)
    ogA = sb("ogA", [128, 2, 512], BF16)
    poS = sb("poS", [128, 2, 512], F32)
    PTA = sb("PTA", [128, 2, 512], BF16)
    PT = sb("PTG", [128, 2, 512], BF16)
    ropeS = sb("ropeS", [128, 128], F32)
    identF = sb("identF", [128, 128], F32)
    identB = sb("identB", [128, 128], BF16)
    qg_bc = sb("qg_bc", [128, 64], F32)
    kg_bc = sb("kg_bc", [128, 64], F32)
    modT = sb("modT", [128, 24, NSEQ], F32)
    Gm = sb("Gm", [128, 8, NSEQ], F32)
    cT = sb("cT_s", [128, 8 * NSEQ], F32)
    scT = sb("scT", [128, 8 * NSEQ], F32)
    badaT = sb("badaT_s", [128, depth * 24], F32)
    ngT = sb("ngT_s", [128, depth * 8], F32)
    ssq = sb("ssq", [128, 1], F32)
    rstd = sb("rstd", [128, 1], F32)
    ssq2 = sb("ssq2", [128, 4], F32)
    rstd2 = sb("rstd2", [128, 4], F32)
    st = sb("st", [128, 8], F32)
    st2 = sb("st2", [128, 8], F32)
    rec = sb("rec", [128, 8], F32)
    recY = sb("recY", [128, 8], F32)
    epsb = sb("epsb", [128, 1], F32)
    oneb = sb("oneb", [128, 1], F32)

    def psb(n, s, d):
        return es.enter_context(nc.psum_tensor(n, s, d))

    P_M = psb("P_M", [128, 512], F32)
    P_A = psb("P_A", [128, 512], F32)
    P_S = [psb(f"P_S{i}", [128, 512], F32) for i in range(2)]
    P_N = psb("P_N", [128, 512], F32)
    P_O = [psb(f"P_O{i}", [128, 512], F32) for i in range(2)]
    P_B = psb("P_B", [128, 512], F32)

    B = {}

    def b(name):
        if name not in B:
            B[name] = Buf(name)
        return B[name]

    state = {"xr": 0, "ps": 0, "pt": 0, "pta": 0, "cast": 0, "xs": 0}

    def next_xr():
        s = state["xr"]
        state["xr"] = (s + 1) % 2
        return s

    fw.dma(sp, lambda: nc.sync.dma_start(out=identF[:], in_=ident_d[:, :]), "c0", writes=[b("identF")])
    fw.op(dve, lambda: V.tensor_copy(out=identB[:], in_=identF[:]), reads=[b("identF")], writes=[b("identB")])
    fw.op(dve, lambda: V.memset(epsb[:], EPS), writes=[b("epsb")])
    fw.op(dve, lambda: V.memset(oneb[:], 1.0), writes=[b("oneb")])
    fw.op(dve, lambda: V.memset(VB[:, :, :, 64:66], 1.0), writes=[b("VBones")])
    fw.op(dve, lambda: V.memset(VAr[:, :, :, 64:66], 1.0), writes=[b("VAones")])
    fw.dma(sp, lambda: nc.sync.dma_start(out=cT[:], in_=cT_d[:, :]), "c1", writes=[b("cT")])
    fw.dma(sp, lambda: nc.sync.dma_start(out=badaT[:], in_=badaT_d[:, :]), "c2", writes=[b("badaT")])
    fw.dma(sp, lambda: nc.sync.dma_start(out=ngT[:], in_=ngT_d[:, :]), "c3", writes=[b("ngT")])
    fw.op(act, lambda: A.activation(out=scT[:], in_=cT[:], func=AF.Exp, scale=-1.0), reads=[b("cT")], writes=[b("scT")])
    fw.op(dve, lambda: V.tensor_scalar_add(out=scT[:], in0=scT[:], scalar1=1.0), reads=[b("scT")], writes=[b("scT")])
    fw.op(dve, lambda: V.reciprocal(out=scT[:], in_=scT[:]), reads=[b("scT")], writes=[b("scT")])
    fw.op(dve, lambda: V.tensor_mul(out=scT[:], in0=scT[:], in1=cT[:]), reads=[b("scT"), b("cT")], writes=[b("scT")])

    def xbuf(s):
        return b(f"xr{s}")

    def cast_op(out_ap, in_ap, reads, writes):
        k = state["cast"]
        state["cast"] = k + 1
        if k % 2 == 0:
            fw.op(dve, lambda: V.tensor_copy(out=out_ap, in_=in_ap), reads=reads, writes=writes)
        else:
            fw.op(pool, lambda: G.tensor_copy(out=out_ap, in_=in_ap), reads=reads, writes=writes)

    def load_cast(dst_ap, src_ap, ncols, wbuf):
        s = next_xr()
        fw.dma(sp, lambda: nc.sync.dma_start(out=xr[:, s, 0:ncols], in_=src_ap), f"xr{s}", writes=[xbuf(s)])
        cast_op(dst_ap, xr[:, s, 0:ncols], [xbuf(s)], [wbuf])

    def headnorm_rope(src_ps, src_buf, H, gbc, gbuf, dst4, dst_bufs, perm, alt=None):
        W = H * 64
        s3 = src_ps.rearrange("p (h d) -> p h d", d=64)
        if alt is None:
            f1, f2, f3 = t1[:, 0:W], t2[:, 0:W], t3[:, 0:W]
            B1, B2, B3 = b("t1"), b("t2"), b("t3")
            st_, st2_, stb, st2b = st[:, 0:H], st2[:, 0:H], b("st"), b("st2")
            rp, rpb_ = ropeS, b("ropeS")
        else:
            tt_, ttb, c0, rp, rpb_ = alt
            f1, f2, f3 = tt_[:, 0:W], tt_[:, 128:128 + W], tt_[:, 256:256 + W]
            B1 = B2 = B3 = ttb
            st_, st2_, stb, st2b = st[:, c0:c0 + H], st2[:, c0:c0 + H], b(f"st{c0}"), b(f"st2{c0}")
        a1 = f1.rearrange("p (h d) -> p h d", d=64)
        a2 = f2.rearrange("p (h d) -> p h d", d=64)
        a3 = f3.rearrange("p (h d) -> p h d", d=64)
        fw.op(act, lambda: A.activation(out=a1, in_=s3, func=AF.Square), reads=[src_buf], writes=[B1])
        fw.op(dve, lambda: V.tensor_reduce(out=st_, in_=a1, axis=AX.X, op=ALU.add), reads=[B1], writes=[stb])
        fw.op(act, lambda: A.activation(out=st2_, in_=st_, func=AF.Ln, scale=1.0 / 64, bias=epsb[:, 0:1]),
              reads=[stb, b("epsb")], writes=[st2b])
        fw.op(act, lambda: A.activation(out=st2_, in_=st2_, func=AF.Exp, scale=-0.5), reads=[st2b], writes=[st2b])
        fw.op(dve, lambda: V.tensor_mul(out=a2, in0=s3, in1=st2_.unsqueeze(2).broadcast_to([128, H, 64])),
              reads=[src_buf, st2b], writes=[B2])
        fw.op(dve, lambda: V.tensor_mul(out=a2, in0=a2, in1=gbc[:].unsqueeze(1).broadcast_to([128, H, 64])),
              reads=[B2, gbuf], writes=[B2])
        fw.op(dve, lambda: V.tensor_mul(out=a1, in0=a2, in1=rp[:, 0:64].unsqueeze(1).broadcast_to([128, H, 64])),
              reads=[B2, rpb_], writes=[B1])
        fns = []
        for g in range(2):
            for f in range(2):
                o = g * 32 + f * 16
                i = g * 32 + (1 - f) * 16
                fns.append(lambda o=o, i=i: V.tensor_mul(out=a3[:, :, o:o + 16], in0=a2[:, :, i:i + 16],
                                                         in1=rp[:, 64 + o:64 + o + 16].unsqueeze(1).broadcast_to([128, H, 16])))
        fw.op(dve, fns, reads=[B2, rpb_], writes=[B3])
        if perm:
            i1 = f1.rearrange("p (k g d) -> p k g d", k=2, g=4)
            i3 = f3.rearrange("p (k g d) -> p k g d", k=2, g=4)
        else:
            i1, i3 = a1, a3
        fw.op(dve, lambda: V.tensor_add(out=dst4, in0=i1, in1=i3), reads=[B1, B3], writes=dst_bufs)

    def bfv(pt):
        return pt[:].bitcast(BF16)

    def norm_front(xs, s, hs, pt, ptb, alt=None):
        if isinstance(xs, tuple):
            xa, xbl = xs
        else:
            xa, xbl = xr[:, xs, :], [xbuf(xs)]
        pv = bfv(pt)
        if alt is None:
            xn, xnb, ssq_, ssqb, rstd_, rstdb = tokbf[:], b("tokbf"), ssq[:], b("ssq"), rstd[:], b("rstd")
        else:
            xn, xnb, ssq_, ssqb, rstd_, rstdb = alt
        fw.op(dve, lambda: V.memset(ssq_, 0.0), writes=[ssqb])
        fw.op(dve, lambda: V.scalar_tensor_tensor(out=xn, in0=xa, scalar=1.0, in1=xa, op0=ALU.mult, op1=ALU.mult,
                                                  accum_out=ssq_), reads=xbl + [ssqb], writes=[xnb, ssqb])
        fw.op(act, lambda: A.activation(out=rstd_, in_=ssq_, func=AF.Ln, scale=1.0 / D, bias=epsb[:, 0:1]),
              reads=[ssqb, b("epsb")], writes=[rstdb])
        fw.op(act, lambda: A.activation(out=rstd_, in_=rstd_, func=AF.Exp, scale=-0.5), reads=[rstdb], writes=[rstdb])
        fw.op(dve, lambda: V.tensor_scalar_mul(out=xn, in0=xa, scalar1=rstd_), reads=xbl + [rstdb], writes=[xnb])
        fw.op(pe, [lambda j=j: T.transpose(pv[:, j * 128:(j + 1) * 128], xn[:, j * 128:(j + 1) * 128], identB[:]) for j in range(8)],
              reads=[xnb, b("identB")], writes=[ptb])
        fw.op(act, [lambda j=j: A.activation(out=hidT[:, hs, j, :], in_=pv[:, j * 128:(j + 1) * 128], func=AF.Identity,
                                             scale=Gm[:, j, s:s + 1], bias=modT[:, j, s:s + 1]) for j in range(8)],
              reads=[ptb, b("Gm"), b("modT")], writes=[b(f"hidT{hs}")])

    def proj_tm(ps_ap, ps_buf, hs, col0, ncols):
        fw.op(pe, [lambda j=j: T.matmul(ps_ap, lhsT=hidT[:, hs, j, :], rhs=wi[:, j, col0:col0 + ncols], start=(j == 0), stop=(j == 7))
                   for j in range(8)], reads=[b(f"hidT{hs}"), b("wi")], writes=[ps_buf])

    def proj_fm(ps_t, ps_buf, hs, col0, nch):
        fns = []
        for c in range(nch):
            for j in range(8):
                fns.append(lambda c=c, j=j: T.matmul(ps_t[:, c * 128:(c + 1) * 128], lhsT=wi[:, j, col0 + c * 128:col0 + (c + 1) * 128],
                                                      rhs=hidT[:, hs, j, :], start=(j == 0), stop=(j == 7)))
        fw.op(pe, fns, reads=[b(f"hidT{hs}"), b("wi")], writes=[ps_buf])

    def denom_from(ps_ap, ps_buf, tt, tbuf, bias_ap=None, bias_buf=None):
        if bias_ap is not None:
            fw.op(dve, lambda: V.tensor_add(out=tt, in0=ps_ap, in1=bias_ap), reads=[ps_buf, bias_buf], writes=[tbuf])
            fw.op(act, lambda: A.activation(out=tt, in_=tt, func=AF.Exp, scale=-1.0), reads=[tbuf], writes=[tbuf])
        else:
            fw.op(act, lambda: A.activation(out=tt, in_=ps_ap, func=AF.Exp, scale=-1.0), reads=[ps_buf], writes=[tbuf])
        fw.op(act, lambda: A.activation(out=tt, in_=tt, func=AF.Ln, scale=1.0, bias=oneb[:, 0:1]), reads=[tbuf, b("oneb")], writes=[tbuf])
        fw.op(act, lambda: A.activation(out=tt, in_=tt, func=AF.Exp, scale=-1.0), reads=[tbuf], writes=[tbuf])

    def merge2(a, c):
        out, ia, ic = [], 0, 0
        na, ncq = len(a), len(c)
        while ia < na or ic < ncq:
            if ic >= ncq or (ia < na and ia * ncq <= ic * na):
                out.append(a[ia])
                ia += 1
            else:
                out.append(c[ic])
                ic += 1
        return out

    def merge_n(qs):
        out = []
        pos = [0] * len(qs)
        tot = sum(len(q) for q in qs)
        while len(out) < tot:
            best, bi = None, -1
            for i, q in enumerate(qs):
                if pos[i] < len(q):
                    frac = pos[i] / len(q)
                    if best is None or frac < best:
                        best, bi = frac, i
            out.append(qs[bi][pos[bi]])
            pos[bi] += 1
        return out

    def chk2(n):
        if STOP is not None and n == STOP:
            fw.finish()
            raise _Stop(nc, fw)

    chk2(0)
    for l in range(depth):
        dst = yout if (depth - 1 - l) % 2 == 0 else scr
        dname = "yout" if dst is yout else "scr"
        if l == 0:
            src, sname = xin, "xin"
        else:
            src = yout if (depth - l) % 2 == 0 else scr
            sname = "yout" if src is yout else "scr"

        for fidx in range(24):
            s = next_xr()
            fw.dma(sp, lambda s=s, fidx=fidx: nc.sync.dma_start(
                out=xr[:, s, :].rearrange("p (k c) -> p k c", c=128),
                in_=wada_d[l, :, fidx * 128:(fidx + 1) * 128].rearrange("(k p) c -> p k c", p=128)), f"xr{s}", writes=[xbuf(s)])
            fw.op(pe, [lambda k=k, s=s, fidx=fidx: T.matmul(P_A[:, fidx * NSEQ:(fidx + 1) * NSEQ],
                                                          lhsT=xr[:, s, k * 128:(k + 1) * 128], rhs=scT[:, k * NSEQ:(k + 1) * NSEQ],
                                                          start=(k == 0), stop=(k == 7)) for k in range(8)],
                  reads=[xbuf(s), b("scT")], writes=[b("P_A")])
        fw.op(dve, lambda: V.tensor_add(out=modT[:], in0=P_A[:, 0:24 * NSEQ].rearrange("p (f s) -> p f s", s=NSEQ),
                                        in1=badaT[:, l * 24:(l + 1) * 24].unsqueeze(2).broadcast_to([128, 24, NSEQ])),
              reads=[b("P_A"), b("badaT")], writes=[b("modT")])
        fw.op(dve, lambda: V.tensor_scalar_add(out=Gm[:], in0=modT[:, 8:16, :], scalar1=1.0), reads=[b("modT")], writes=[b("Gm")])
        fw.op(dve, lambda: V.tensor_mul(out=Gm[:], in0=Gm[:], in1=ngT[:, l * 8:(l + 1) * 8].unsqueeze(2).broadcast_to([128, 8, NSEQ])),
              reads=[b("Gm"), b("ngT")], writes=[b("Gm")])

        chk2(1)
        pieces = [(0, 1024), (1024, 1024), (2048, 1024), (3072, 1024), (4096, 1024), (5120, 256)]
        for k in range(8):
            for (c0, n) in pieces:
                load_cast(wi[:, k, c0:c0 + n], win_d[l, k * 128:(k + 1) * 128, c0:c0 + n], n, b("wi"))
        for k in range(4):
            load_cast(wpa[:, k, :], wpa_d[l, k * 128:(k + 1) * 128, :], D, b("wpa"))
            load_cast(wpb[:, k, :], wpb_d[l, k * 128:(k + 1) * 128, :], D, b("wpb"))
        for h2 in range(2):
            load_cast(bg[:, h2 * D:(h2 + 1) * D], bgate_d[l, h2 * D:(h2 + 1) * D].partition_broadcast(128), D, b("bg"))
        fw.dma(sp, lambda: nc.sync.dma_start(out=qg_bc[:], in_=qg_d[l, :].partition_broadcast(128)), "c4", writes=[b("qg")])
        fw.dma(sp, lambda: nc.sync.dma_start(out=kg_bc[:], in_=kg_d[l, :].partition_broadcast(128)), "c5", writes=[b("kg")])
        chk2(2)
        for t9 in range(9):
            s = next_xr()
            fw.dma(sp, lambda s=s, t9=t9: nc.sync.dma_start(out=xr[:, s, :], in_=rpbg_d[l, t9, :, :]), f"xr{s}", writes=[xbuf(s)])
            fw.op(act, lambda s=s, t9=t9: A.activation(out=E[:, t9, :], in_=xr[:, s, :], func=AF.Exp), reads=[xbuf(s)], writes=[b("E")])

        chk2(3)
        off = 0
        for s_i, N in enumerate(seq_lens):
            NT = N // 128
            sq = s_i

            def dbuf(name, t):
                return b(f"{name}:{off // 128 + t}")

            def load_x(t, which, nm):
                s = next_xr()
                r0 = off + t * 128
                fw.dma(sp, lambda: nc.sync.dma_start(out=xr[:, s, :], in_=which[r0:r0 + 128, :]), f"xr{s}",
                       reads=[dbuf(nm, t)], writes=[xbuf(s)])
                return s

            def load_rope(t):
                fw.dma(sp, lambda: nc.sync.dma_start(out=ropeS[:], in_=rope_d[t * 128:(t + 1) * 128, :]), "rope", writes=[b("ropeS")])

            for half in range(2):
                pst, pbuf = (P_A, b("P_A")) if half == 0 else (P_B, b("P_B"))
                tg, tgb = (t1, b("t1")) if half == 0 else (t2, b("t2"))
                for jj in range(4):
                    j = half * 4 + jj
                    fw.op(dve, lambda: V.memset(t3[:, 0:128], 1.0), writes=[b("t3")])
                    fw.op(dve, lambda j=j: V.tensor_scalar_mul(out=t3[:, 128:256], in0=identF[:], scalar1=modT[:, 16 + j, sq:sq + 1]),
                          reads=[b("identF"), b("modT")], writes=[b("t3")])
                    fw.op(pe, lambda jj=jj, pst=pst: T.matmul(pst[:, jj * 128:(jj + 1) * 128], lhsT=t3[:, 0:128], rhs=t3[:, 128:256], start=True, stop=True),
                          reads=[b("t3")], writes=[pbuf])
                fw.op(dve, lambda pst=pst, tg=tg: V.tensor_copy(out=tg[:], in_=pst[:]), reads=[pbuf], writes=[tgb])
            for k in range(8):
                s = next_xr()
                fw.dma(sp, lambda s=s, k=k: nc.sync.dma_start(out=xr[:, s, :], in_=wout_d[l, k * 128:(k + 1) * 128, :]), f"xr{s}", writes=[xbuf(s)])
                fw.op(dve, lambda s=s, k=k: V.tensor_mul(out=wo[:, k, 0:512], in0=xr[:, s, 0:512], in1=t1[:]), reads=[xbuf(s), b("t1")], writes=[b("wo")])
                fw.op(pool, lambda s=s, k=k: G.tensor_mul(out=wo[:, k, 512:1024], in0=xr[:, s, 512:1024], in1=t2[:]), reads=[xbuf(s), b("t2")], writes=[b("wo")])

            chk2(4)
            fbv = featbf[:].rearrange("p c t -> p (c t)")
            ogv = ogA[:].rearrange("p a t -> p (a t)")
            qbv = qBT[:].rearrange("p a c t -> p (a c t)")
            NCH = 3
            katf = KAT[:].rearrange("p c t -> p (c t)")[:, 0:2048].bitcast(F32)
            katb = [b(f"KAT{r_}") for r_ in range(RING)]
            xslots = [(xr[:, 0, :], [xbuf(0)], "xr0"), (xr[:, 1, :], [xbuf(1)], "xr1"), (katf, katb, "xk")]
            sets = [
                (P_M, "P_M", P_A, "P_A", tokbf[:], "tokbf", t1[:], "t1", ropeS[:], "ropeS"),
                (P_N, "P_N", P_B, "P_B", fbv, "featbf", t2[:], "t2", t3[:, 0:128], "t3a"),
                (P_S[0], "P_S0", P_O[0], "P_O0", ogv, "ogAall", poS[:, 0, :], "poS0", t3[:, 128:256], "t3b"),
                (P_S[1], "P_S1", P_O[1], "P_O1", qbv, "qBTall", poS[:, 1, :], "poS1", t3[:, 256:384], "t3c"),
            ]

            def pass1_tile(t):
                e = t % NCH
                pt, ptn, pj, pjn, xnv, xnn, hsc, hscn, rpt, rpn = sets[e]
                ptb, pjb, xnb = b(ptn), b(pjn), b(xnn)
                xap, xbl, xkey = xslots[e]
                r0 = off + t * 128
                fw.dma(sp, lambda: nc.sync.dma_start(out=xap, in_=src[r0:r0 + 128, :]), xkey, reads=[dbuf(sname, t)], writes=xbl)
                xs = (xap, xbl)
                hs = e
                fw.dma(sp, lambda: nc.sync.dma_start(out=rpt, in_=rope_d[t * 128:(t + 1) * 128, :]), f"rope{e}", writes=[b(rpn)])
                nalt = (xnv, xnb, ssq2[:, e:e + 1], b(f"ssq2{e}"), rstd2[:, e:e + 1], b(f"rstd2{e}"))
                norm_front(xs, sq, hs, pt, ptb, nalt)
                proj_tm(pj[:, 0:256], pjb, hs, KB, 256)
                fw.op(act, lambda: A.copy(out=VB[:, t, :, 0:64], in_=pj[:, 128:256].rearrange("p (h d) -> p h d", d=64)),
                      reads=[pjb], writes=[b("VB")])
                kst = xnv[:, 0:128]
                headnorm_rope(pj[:, 0:128], pjb, 2, kg_bc, b("kg"), kst.rearrange("p (h d) -> p h d", d=64), [xnb], False,
                              alt=(hsc, b(hscn), 2 * e, rpt, b(rpn)))
                fw.op(pe, lambda: T.transpose(bfv(pt)[:, 0:128], kst, identB[:]), reads=[xnb, b("identB")], writes=[ptb])
                fw.op(dve, lambda: V.tensor_copy(out=KBT[:, t * 128:(t + 1) * 128], in_=bfv(pt)[:, 0:128]), reads=[ptb], writes=[b(f"KBT{e}")])

            qs = [[] for _ in range(NCH)]
            for t in range(NT):
                fw.begin_defer()
                pass1_tile(t)
                qs[t % NCH].extend(fw.end_defer())
            for th in merge_n(qs):
                th()

            chk2(5)
            ya1 = poS[:, 0, :]
            ya2 = poS[:, 1, :]

            def FE(t, alt2=False):
                xs = load_x(t, src, sname)
                hs = t % 4
                if alt2:
                    pa, pab, pbk, pbb = P_N, b("P_N"), P_M, b("P_M")
                    nalt = (fbv, b("featbf"), ssq2[:, 0:1], b("ssq20"), rstd2[:, 0:1], b("rstd20"))
                else:
                    pa, pab, pbk, pbb = P_A, b("P_A"), P_B, b("P_B")
                    nalt = None
                norm_front(xs, sq, hs, pa, pab, nalt)
                rs = t % RING
                proj_fm(pbk, pbb, hs, KA, 4)
                fw.op(dve, lambda: V.tensor_copy(out=KAT[:, :, rs * 128:(rs + 1) * 128], in_=pbk[:].rearrange("p (c t) -> p c t", t=128)),
                      reads=[pbb], writes=[b(f"KAT{rs}")])
                proj_tm(pa[:, 0:512], pab, hs, VA, 512)
                fw.op(dve, lambda: V.tensor_copy(out=VAr[:, rs, :, 0:64], in_=pa[:].rearrange("p (h d) -> p h d", d=64)),
                      reads=[pab], writes=[b(f"VAr{rs}")])

            def normalize_part(hg, pt_, pbuf_, strided, rc, rcb, o2, o2b):
                fw.op(dve, lambda: V.reciprocal(out=rc[:, hg * 4:(hg + 1) * 4].unsqueeze(2),
                                                in_=pt_[:, 0:264].rearrange("p (h e) -> p h e", e=66)[:, :, 64:65]),
                      reads=[pbuf_], writes=[rcb])
                fw.op(dve, lambda: V.tensor_mul(
                    out=(o2.rearrange("p (c two d) -> p two c d", two=2, d=64)[:, hg] if strided
                         else o2[:, hg * 256:(hg + 1) * 256].rearrange("p (h d) -> p h d", d=64)),
                    in0=pt_[:, 0:264].rearrange("p (h e) -> p h e", e=66)[:, :, 0:64],
                    in1=rc[:, hg * 4:(hg + 1) * 4].unsqueeze(2).broadcast_to([128, 4, 64])),
                    reads=[pbuf_, rcb], writes=[o2b])

            def zgate(hs, zcol, zbank, zbuf, s1, s1b, o2, o2b, out_ap, out_buf):
                proj_tm(zbank[:, 0:512], zbuf, hs, zcol, 512)
                denom_from(zbank[:, 0:512], zbuf, s1, s1b)
                fw.op(dve, lambda: V.tensor_mul(out=s1, in0=zbank[:, 0:512], in1=s1), reads=[s1b, zbuf], writes=[s1b])
                fw.op(dve, lambda: V.tensor_mul(out=out_ap, in0=o2, in1=s1), reads=[s1b, o2b], writes=[out_buf])

            def Z1(t):
                hs = t % 4
                tb = t % 2
                proj_fm(P_N, b("P_N"), hs, QA, 4)
                fw.op(dve, lambda: V.tensor_copy(out=qAT[:], in_=P_N[:].rearrange("p (c t) -> p c t", t=128)), reads=[b("P_N")], writes=[b("qAT")])
                load_rope(t)
                proj_tm(P_M[:, 0:512], b("P_M"), hs, QB, 512)
                headnorm_rope(P_M[:, 0:512], b("P_M"), 8, qg_bc, b("qg"),
                              PTA[:, 0, :].rearrange("p (g k d) -> p k g d", g=4, k=2), [b("PTA0")], True)
                pvn = bfv(P_N)
                fw.op(pe, [lambda g=g: T.transpose(pvn[:, g * 128:(g + 1) * 128], PTA[:, 0, g * 128:(g + 1) * 128], identB[:]) for g in range(4)],
                      reads=[b("PTA0"), b("identB")], writes=[b("P_N")])
                fw.op(dve, lambda: V.tensor_copy(out=qBT[:, tb], in_=pvn[:, 0:512].rearrange("p (c t) -> p c t", t=128)),
                      reads=[b("P_N")], writes=[b(f"qBT{tb}")])

            def Z2(t):
                hs = t % 4
                tb = t % 2
                kts, tabs = na_plan(t, NT)
                nk = len(kts)
                items = [(par, idx, kt, tab) for par in range(2) for idx, (kt, tab) in enumerate(zip(kts, tabs))]

                def NA_S(n):
                    par, idx, kt, tab = items[n]
                    pb = par * 64
                    rs = kt % RING
                    fw.op(pe, [lambda hh=hh: T.matmul(
                        P_M[:, hh * 128:(hh + 1) * 128], lhsT=KAT[pb:pb + 64, hh, rs * 128:(rs + 1) * 128],
                        rhs=qAT[pb:pb + 64, hh, :], start=True, stop=True) for hh in range(4)],
                        reads=[b(f"KAT{rs}"), b("qAT")], writes=[b("P_M")])

                def NA_exp(n):
                    sl = n % 2
                    fw.op(act, lambda: A.activation(out=PTA[:, sl, :], in_=P_M[:], func=AF.Exp, scale=0.125),
                          reads=[b("P_M")], writes=[b(f"PTA{sl}")])

                def NA_rest(n):
                    par, idx, kt, tab = items[n]
                    rs = kt % RING
                    sl = n % 2
                    fw.op(dve, lambda: V.tensor_mul(
                        out=PTA[:, sl, :].rearrange("p (c q) -> p c q", q=128), in0=PTA[:, sl, :].rearrange("p (c q) -> p c q", q=128),
                        in1=E[:, tab, :].rearrange("p (c two q) -> p two c q", two=2, q=128)[:, par]),
                        reads=[b(f"PTA{sl}"), b("E")], writes=[b(f"PTA{sl}")])
                    fw.op(pe, [lambda hh=hh: T.matmul(
                        P_N[:, hh * 66:(hh + 1) * 66], lhsT=PTA[:, sl, hh * 128:(hh + 1) * 128], rhs=VAr[:, rs, 2 * hh + par, :],
                        start=(idx == 0 and hh == 0), stop=(idx == nk - 1 and hh == 3)) for hh in range(4)],
                        reads=[b(f"PTA{sl}"), b(f"VAr{rs}"), b("VAones")], writes=[b("P_N")])
                    if idx == nk - 1:
                        normalize_part(par, P_N, b("P_N"), True, rec, b("rec"), t2[:], b("t2"))

                NA_S(0)
                for n in range(len(items)):
                    NA_exp(n)
                    if n + 1 < len(items):
                        NA_S(n + 1)
                    NA_rest(n)
                zgate(hs, ZA, P_M, b("P_M"), t1[:], b("t1"), t2[:], b("t2"), ogA[:, tb, :], b(f"ogA{tb}"))

            def P567(t):
                hs = t % 4
                tb = t % 2
                for kvh in range(2):
                    ptr, ptrb = (P_A, b("P_A")) if kvh == 0 else (P_B, b("P_B"))
                    fw.op(pe, [lambda g=g, ptr=ptr, kvh=kvh: T.transpose(ptr[:, g * 66:(g + 1) * 66], poS[0:66, kvh, g * 128:(g + 1) * 128], identF[0:66, 0:66])
                               for g in range(4)], reads=[b(f"poS{kvh}"), b("identF")], writes=[ptrb])
                normalize_part(0, P_A, b("P_A"), False, recY, b("recY"), ya2, b("poS1"))
                normalize_part(1, P_B, b("P_B"), False, recY, b("recY"), ya2, b("poS1"))
                zgate(hs, ZB, P_A, b("P_A"), ya1, b("poS0"), ya2, b("poS1"), tokbf[:, 512:1024], b("tokbf"))
                pvb = bfv(P_B)
                fw.op(pe, [lambda c=c: T.transpose(pvb[:, c * 128:(c + 1) * 128], ogA[:, tb, c * 128:(c + 1) * 128], identB[:]) for c in range(4)] +
                          [lambda c=c: T.transpose(pvb[:, c * 128:(c + 1) * 128], tokbf[:, c * 128:(c + 1) * 128], identB[:]) for c in range(4, 8)],
                      reads=[b(f"ogA{tb}"), b("tokbf"), b("identB")], writes=[b("P_B")])
                fw.op(dve, lambda: V.tensor_copy(out=featbf[:], in_=pvb.rearrange("p (c t) -> p c t", t=128)), reads=[b("P_B")], writes=[b("featbf")])
                for half in range(2):
                    cs = slice(half * 512, (half + 1) * 512)
                    proj_tm(P_A[:, 0:512], b("P_A"), hs, GA + half * 512, 512)
                    fw.op(pe, [lambda c=c, cs=cs: T.matmul(P_B[:, 0:512], lhsT=featbf[:, c, :], rhs=wpa[:, c, cs], start=(c == 0), stop=(c == 3))
                               for c in range(4)], reads=[b("featbf"), b("wpa")], writes=[b("P_B")])
                    denom_from(P_A[:, 0:512], b("P_A"), ya1, b("poS0"), bg[:, half * 512:(half + 1) * 512], b("bg"))
                    proj_tm(P_A[:, 0:512], b("P_A"), hs, GB + half * 512, 512)
                    fw.op(dve, lambda: V.tensor_mul(out=ya1, in0=P_B[:, 0:512], in1=ya1), reads=[b("P_B"), b("poS0")], writes=[b("poS0")])
                    fw.op(pe, [lambda c=c, cs=cs: T.matmul(P_B[:, 0:512], lhsT=featbf[:, 4 + c, :], rhs=wpb[:, c, cs], start=(c == 0), stop=(c == 3))
                               for c in range(4)], reads=[b("featbf"), b("wpb")], writes=[b("P_B")])
                    denom_from(P_A[:, 0:512], b("P_A"), ya2, b("poS1"), bg[:, D + half * 512:D + (half + 1) * 512], b("bg"))
                    fw.op(dve, lambda: V.tensor_mul(out=ya2, in0=P_B[:, 0:512], in1=ya2), reads=[b("P_B"), b("poS1")], writes=[b("poS1")])
                    fw.op(dve, lambda cs=cs: V.tensor_add(out=tokbf[:, cs], in0=ya1, in1=ya2), reads=[b("poS0"), b("poS1")], writes=[b("tokbf")])
                pva = bfv(P_A)
                fw.op(pe, [lambda c=c: T.transpose(pva[:, c * 128:(c + 1) * 128], tokbf[:, c * 128:(c + 1) * 128], identB[:]) for c in range(8)],
                      reads=[b("tokbf"), b("identB")], writes=[b("P_A")])
                fw.op(dve, lambda: V.tensor_copy(out=featbf[:], in_=pva.rearrange("p (c t) -> p c t", t=128)), reads=[b("P_A")], writes=[b("featbf")])
                xs = load_x(t, src, sname)
                for half in range(2):
                    cs = slice(half * 512, (half + 1) * 512)
                    pst, pbuf = (P_B, b("P_B")) if half == 0 else (P_A, b("P_A"))
                    fw.op(pe, [lambda j=j, cs=cs, pst=pst: T.matmul(pst[:, 0:512], lhsT=featbf[:, j, :], rhs=wo[:, j, cs], start=(j == 0), stop=(j == 7))
                               for j in range(8)], reads=[b("featbf"), b("wo")], writes=[pbuf])
                    fw.op(dve, lambda cs=cs, pst=pst: V.tensor_add(out=xr[:, xs, cs], in0=xr[:, xs, cs], in1=pst[:, 0:512]),
                          reads=[xbuf(xs), pbuf], writes=[xbuf(xs)])
                r0 = off + t * 128
                fw.dma(pool, lambda: nc.gpsimd.dma_start(out=dst[r0:r0 + 128, :], in_=xr[:, xs, :]), f"st{xs}",
                       reads=[xbuf(xs)], writes=[dbuf(dname, t)])

            def GQA(t):
                tb = t % 2
                for kvh in range(2):
                    pb = kvh * 64
                    po, pob = P_O[kvh], b(f"P_O{kvh}")

                    def S(kt, pb=pb):
                        pi = kt % 2
                        fw.op(pe, lambda: T.matmul(P_S[pi][:, 0:512], lhsT=KBT[pb:pb + 64, kt * 128:(kt + 1) * 128],
                                                   rhs=qBT[pb:pb + 64, tb].rearrange("p c t -> p (c t)"), start=True, stop=True),
                              reads=[b("KBT0"), b("KBT1"), b("KBT2"), b(f"qBT{tb}")], writes=[b(f"P_S{pi}")])

                    S(0)
                    for kt in range(NT):
                        if kt + 1 < NT:
                            S(kt + 1)
                        pi = kt % 2
                        fw.op(act, lambda pi=pi: A.activation(out=PT[:, pi, :], in_=P_S[pi][:], func=AF.Exp, scale=0.125),
                              reads=[b(f"P_S{pi}")], writes=[b(f"PT{pi}")])
                        fw.op(pe, lambda pi=pi, kt=kt, po=po, kvh=kvh: T.matmul(po[0:66, 0:512], lhsT=VB[:, kt, kvh, :], rhs=PT[:, pi, :],
                                                                start=(kt == 0), stop=(kt == NT - 1)),
                              reads=[b(f"PT{pi}"), b("VB"), b("VBones")], writes=[pob])

            def run_merged(gq, xq):
                for th in merge2(gq, xq):
                    th()

            fw.begin_defer()
            for t in range(0, min(4, NT), 2):
                state["xr"] = 0
                FE(t)
            qa_ = fw.end_defer()
            fw.begin_defer()
            for t in range(1, min(4, NT), 2):
                state["xr"] = 1
                FE(t, alt2=True)
            qb_ = fw.end_defer()
            for th in merge2(qa_, qb_):
                th()
            Z1(0)
            chk2(6)
            for t in range(NT):
                def dq(*calls):
                    fw.begin_defer()
                    for c in calls:
                        c()
                    return fw.end_defer()
                y1 = dq(*([lambda: P567(t - 1)] if t >= 1 else []))
                z2 = dq(lambda: Z2(t))
                y2 = dq(*([lambda: FE(t + 3)] if (t >= 1 and t + 3 < NT) else []))
                z1 = dq(*([lambda: Z1(t + 1)] if t + 1 < NT else []))
                gq = dq(lambda: GQA(t))
                run_merged(gq, merge2(y1, z2) + merge2(y2, z1))
                for kvh in range(2):
                    fw.op(act, lambda kvh=kvh: A.copy(out=poS[0:66, kvh, :], in_=P_O[kvh][0:66, :]), reads=[b(f"P_O{kvh}")], writes=[b(f"poS{kvh}")])
            P567(NT - 1)
            chk2(11)
            off += N

    fgb = E[:, 0:2, :].rearrange("p a b -> p (a b)").bitcast(F32)
    fw.dma(sp, lambda: nc.sync.dma_start(out=fgb, in_=fg_d.partition_broadcast(128)), "c6", writes=[b("E")])
    for tt in range(TOT // 128):
        s = next_xr()
        r0 = tt * 128
        fw.dma(sp, lambda: nc.sync.dma_start(out=xr[:, s, :], in_=yout[r0:r0 + 128, :]), f"xr{s}", reads=[b(f"yout:{tt}")], writes=[xbuf(s)])
        xa = xr[:, s, :]
        fw.op(dve, lambda: V.memset(ssq[:], 0.0), writes=[b("ssq")])
        fw.op(dve, lambda: V.scalar_tensor_tensor(out=tokbf[:], in0=xa, scalar=1.0, in1=xa, op0=ALU.mult, op1=ALU.mult, accum_out=ssq[:]),
              reads=[xbuf(s), b("ssq")], writes=[b("tokbf"), b("ssq")])
        fw.op(act, lambda: A.activation(out=rstd[:], in_=ssq[:], func=AF.Ln, scale=1.0 / D, bias=epsb[:, 0:1]), reads=[b("ssq"), b("epsb")], writes=[b("rstd")])
        fw.op(act, lambda: A.activation(out=rstd[:], in_=rstd[:], func=AF.Exp, scale=-0.5), reads=[b("rstd")], writes=[b("rstd")])
        fw.op(dve, lambda: V.scalar_tensor_tensor(out=xa, in0=xa, scalar=rstd[:, 0:1], in1=fgb, op0=ALU.mult, op1=ALU.mult),
              reads=[xbuf(s), b("rstd"), b("E")], writes=[xbuf(s)])
        fw.dma(pool, lambda: nc.gpsimd.dma_start(out=yout[r0:r0 + 128, :], in_=xr[:, s, :]), f"st{s}", reads=[xbuf(s)], writes=[b(f"yout:{tt}")])
    fw.finish()
    return nc, fw


def _const_tables(nmax):
    t = np.arange(nmax)
    row = (t // 64).astype(np.float32)
    col = (t % 64).astype(np.float32)
    inv = (10000.0 ** (-np.arange(0, 32, 2, dtype=np.float32) / 32)).astype(np.float32)
    ar = row[:, None] * inv
    ac = col[:, None] * inv
    C = np.concatenate([np.cos(ar), np.cos(ar), np.cos(ac), np.cos(ac)], axis=1)
    S = np.concatenate([-np.sin(ar), np.sin(ar), -np.sin(ac), np.sin(ac)], axis=1)
    return np.ascontiguousarray(np.concatenate([C, S], axis=1).astype(np.float32))


def _rpb_gather_index():
    kr = np.arange(128) // 64
    kc = np.arange(128) % 64
    qr, qc = kr, kc
    cs = np.clip(qc - 8, 0, 48)
    colok = (kc[:, None] >= cs[None, :]) & (kc[:, None] < cs[None, :] + 16)
    coff = kc[:, None] - qc[None, :] + 15
    idx = np.full((9, 128, 128), 465, dtype=np.int64)
    specs = [(-3, None), (-2, None), (-1, None), (0, None), (1, None), (2, None), (3, None), (-2, "int"), (2, "int")]
    for ti, (dt_, kind) in enumerate(specs):
        dr = 2 * dt_ + kr[:, None] - qr[None, :]
        ok = colok & (dr + 7 >= 0) & (dr + 7 <= 14)
        if kind == "int":
            ok = ok & (dr >= -4) & (dr <= 3)
        flat = (np.clip(dr + 7, 0, 14)) * 31 + np.clip(coff, 0, 30)
        idx[ti] = np.where(ok, flat, 465)
    return idx


def _host_inputs(core_seqs_x, core_c, W, depth, nmax):
    nseq = len(core_seqs_x)
    xin = np.ascontiguousarray(np.concatenate(core_seqs_x, axis=0))
    cmat = np.stack(core_c, axis=0)
    cT = np.ascontiguousarray(cmat.reshape(nseq, 8, 128).transpose(2, 1, 0).reshape(128, 8 * nseq))
    m = dict(W)
    m["xin"] = xin
    m["cT"] = cT
    return m


def _shared_inputs(norm_g, w_ada, b_ada, w_in, b_gate, rpb, q_norm_g, k_norm_g, w_pa, w_pb, w_out, final_g, depth, nmax):
    ngT = np.ascontiguousarray(norm_g[:depth].reshape(depth, 8, 128).transpose(2, 0, 1).reshape(128, depth * 8))
    badaT = np.ascontiguousarray(b_ada[:depth].reshape(depth, 24, 128).transpose(2, 0, 1).reshape(128, depth * 24))
    idx = _rpb_gather_index()
    rp = rpb[:depth].reshape(depth, 8, 465)
    rp = np.concatenate([rp, np.full((depth, 8, 1), NEG, np.float32)], axis=2)
    g = rp[:, :, idx]
    rpbg = np.ascontiguousarray(g.transpose(0, 2, 3, 1, 4).reshape(depth, 9, 128, 1024))
    return {
        "ngT": ngT.astype(np.float32), "badaT": badaT.astype(np.float32),
        "w_ada": np.ascontiguousarray(w_ada[:depth]), "w_in": np.ascontiguousarray(w_in[:depth]),
        "b_gate": np.ascontiguousarray(b_gate[:depth]), "rpbg": rpbg.astype(np.float32),
        "q_norm_g": np.ascontiguousarray(q_norm_g[:depth]), "k_norm_g": np.ascontiguousarray(k_norm_g[:depth]),
        "w_pa": np.ascontiguousarray(w_pa[:depth]), "w_pb": np.ascontiguousarray(w_pb[:depth]),
        "w_out": np.ascontiguousarray(w_out[:depth]), "final_g": np.ascontiguousarray(final_g),
        "rope": _const_tables(nmax), "ident": np.eye(128, dtype=np.float32),
    }


def kernel(x_prompt, x_sample, c_prompt, c_sample, norm_g, w_ada, b_ada, w_in, b_gate, rpb,
           q_norm_g, k_norm_g, w_pa, w_pb, w_out, final_g):
    f = lambda a: np.asarray(a, dtype=np.float32)
    x_prompt, x_sample, c_prompt, c_sample = f(x_prompt), f(x_sample), f(c_prompt), f(c_sample)
    n = 8
    depth = int(np.asarray(w_in).shape[0])
    bp, sp_, _ = x_prompt.shape
    bs, ss, _ = x_sample.shape
    pp, ps_ = bp // n, bs // n
    seq_lens = [sp_] * pp + [ss] * ps_
    nmax = max(seq_lens)
    W = _shared_inputs(f(norm_g), f(w_ada), f(b_ada), f(w_in), f(b_gate), f(rpb), f(q_norm_g), f(k_norm_g),
                       f(w_pa), f(w_pb), f(w_out), f(final_g), depth, nmax)
    nc, _ = build_program(seq_lens, depth)
    in_maps = []
    for c in range(n):
        xs = [x_prompt[c * pp + i] for i in range(pp)] + [x_sample[c * ps_ + i] for i in range(ps_)]
        cs = [c_prompt[c * pp + i] for i in range(pp)] + [c_sample[c * ps_ + i] for i in range(ps_)]
        in_maps.append(_host_inputs(xs, cs, W, depth, nmax))
    res = run_bass_kernel_spmd(nc, in_maps, core_ids=list(range(n)))
    y_prompt = np.empty_like(x_prompt)
    y_sample = np.empty_like(x_sample)
    for c in range(n):
        y = res.results[c]["yout"]
        o = 0
        for i in range(pp):
            y_prompt[c * pp + i] = y[o:o + sp_]
            o += sp_
        for i in range(ps_):
            y_sample[c * ps_ + i] = y[o:o + ss]
            o += ss
    return (y_prompt, y_sample)
```

```python
import numpy as np
from contextlib import ExitStack
import concourse.bass as bass
import concourse.mybir as mybir
from concourse.bass_utils import run_bass_kernel_spmd

F32 = mybir.dt.float32
BF16 = mybir.dt.bfloat16
AF = mybir.ActivationFunctionType
ALU = mybir.AluOpType
AX = mybir.AxisListType

D = 1024
DIN = 5376
QA, KA, VA, ZA, QB, KB, VB_, ZB, GA, GB = 0, 512, 1024, 1536, 2048, 2560, 2688, 2816, 3328, 4352
EPS = 1e-6
NEG = -30000.0
RING = 5


class Buf:
    __slots__ = ("name", "w", "r")

    def __init__(self, name):
        self.name = name
        self.w = None
        self.r = {}


class Eng:
    def __init__(self, name, h, sem, in_order):
        self.name, self.h, self.sem, self.in_order = name, h, sem, in_order
        self.cnt = 0
        self.waited = {}


class FW:
    def __init__(self, nc, es):
        self.nc, self.es = nc, es
        mk = lambda n: es.enter_context(nc.semaphore(n))
        self.pe = Eng("pe", nc.tensor, mk("s_pe"), True)
        self.act = Eng("act", nc.scalar, mk("s_act"), False)
        self.dve = Eng("dve", nc.vector, mk("s_dve"), False)
        self.pool = Eng("pool", nc.gpsimd, mk("s_pool"), False)
        self.sp = Eng("sp", nc.sync, mk("s_sp"), False)
        self.engs = [self.pe, self.act, self.dve, self.pool, self.sp]
        self.dma_sems = {}
        self.ninst = 0
        self.q = None

    def begin_defer(self):
        self.q = []

    def end_defer(self):
        q, self.q = self.q, None
        return q

    def _deps(self, eng, reads, writes, is_dma):
        need = {}

        def add(tok, raw):
            sem, val, src = tok
            if src is eng and not is_dma:
                if eng.in_order:
                    return
            if eng.waited.get(sem, 0) >= val:
                return
            if need.get(sem, 0) < val:
                need[sem] = val

        for b in reads:
            if b.w is not None:
                add(b.w, True)
        for b in writes:
            if b.w is not None:
                add(b.w, False)
            for t in b.r.values():
                add(t, False)
        return list(need.items())

    def _emit(self, eng, fns, need, inc_sem, inc_val):
        inline = need.pop() if need else None
        for s, v in need:
            eng.h.wait_ge(s, v)
            eng.waited[s] = v
        last = None
        first = True
        for fn in fns:
            ins = fn()
            self.ninst += 1
            if first and inline is not None:
                ins.wait_op(inline[0], inline[1], "sem-ge")
                eng.waited[inline[0]] = inline[1]
            first = False
            last = ins
        last.then_inc(inc_sem, inc_val)

    def _post(self, tok, reads, writes):
        for b in reads:
            b.r[tok[0]] = tok
        for b in writes:
            b.w = tok
            b.r = {}

    def op(self, eng, fns, reads=(), writes=()):
        if self.q is not None:
            self.q.append(lambda: self._op(eng, fns, reads, writes))
            return None
        return self._op(eng, fns, reads, writes)

    def _op(self, eng, fns, reads=(), writes=()):
        if not isinstance(fns, (list, tuple)):
            fns = [fns]
        need = self._deps(eng, reads, writes, False)
        self._emit(eng, fns, need, eng.sem, 1)
        eng.cnt += 1
        tok = (eng.sem, eng.cnt, eng)
        self._post(tok, reads, writes)
        return tok

    def dma(self, eng, fn, key, reads=(), writes=()):
        if self.q is not None:
            self.q.append(lambda: self._dma(eng, fn, key, reads, writes))
            return None
        return self._dma(eng, fn, key, reads, writes)

    def _dma(self, eng, fn, key, reads=(), writes=()):
        if key not in self.dma_sems:
            self.dma_sems[key] = [self.es.enter_context(self.nc.semaphore("d_" + key)), 0]
        ent = self.dma_sems[key]
        need = dict(self._deps(eng, reads, writes, True))
        if ent[1] > 0 and eng.waited.get(ent[0], 0) < ent[1]:
            need[ent[0]] = ent[1]
        self._emit(eng, [fn], list(need.items()), ent[0], 16)
        ent[1] += 16
        tok = (ent[0], ent[1], None)
        self._post(tok, reads, writes)
        return tok

    def finish(self):
        for e in self.engs:
            if e is not self.sp and e.cnt > 0:
                self.sp.h.wait_ge(e.sem, e.cnt)
        for k, ent in self.dma_sems.items():
            if ent[1] > 0:
                self.sp.h.wait_ge(ent[0], ent[1])


def na_plan(i, I):
    if i < 2:
        kts = [0, 1, 2, 3]
        return kts, [kt - i + 3 for kt in kts]
    if i >= I - 2:
        kts = [I - 4, I - 3, I - 2, I - 1]
        return kts, [kt - i + 3 for kt in kts]
    return [i - 2, i - 1, i, i + 1, i + 2], [7, 2, 3, 4, 8]


class _Stop(Exception):
    pass


STOP = None


def chk(n):
    if STOP is not None and n == STOP:
        raise _Stop()


def build_program(seq_lens, depth):
    try:
        return _build_program(seq_lens, depth)
    except _Stop as e:
        nc, fw = e.args
        return nc, fw


def _build_program(seq_lens, depth):
    NSEQ = len(seq_lens)
    TOT = sum(seq_lens)
    NMAX = max(seq_lens)
    nc = bass.Bass("TRN2", target_bir_lowering=False)
    es = ExitStack()
    fw = FW(nc, es)
    pe, act, dve, pool, sp = fw.pe, fw.act, fw.dve, fw.pool, fw.sp
    V, A, T, G = nc.vector, nc.scalar, nc.tensor, nc.gpsimd

    def dram(n, s, d=F32, k="ExternalInput"):
        return nc.dram_tensor(n, s, d, kind=k).ap()

    xin = dram("xin", [TOT, D])
    yout = dram("yout", [TOT, D], k="ExternalOutput")
    scr = dram("scr", [TOT, D], k="Internal")
    cT_d = dram("cT", [128, 8 * NSEQ])
    ngT_d = dram("ngT", [128, depth * 8])
    badaT_d = dram("badaT", [128, depth * 24])
    wada_d = dram("w_ada", [depth, D, 3 * D])
    win_d = dram("w_in", [depth, D, DIN])
    bgate_d = dram("b_gate", [depth, 2 * D])
    rpbg_d = dram("rpbg", [depth, 9, 128, 1024])
    qg_d = dram("q_norm_g", [depth, 64])
    kg_d = dram("k_norm_g", [depth, 64])
    wpa_d = dram("w_pa", [depth, 512, D])
    wpb_d = dram("w_pb", [depth, 512, D])
    wout_d = dram("w_out", [depth, D, D])
    fg_d = dram("final_g", [D])
    rope_d = dram("rope", [NMAX, 128])
    ident_d = dram("ident", [128, 128])

    def sb(n, s, d):
        return es.enter_context(nc.sbuf_tensor(n, s, d))

    wi = sb("wi", [128, 8, DIN], BF16)
    wpa = sb("wpa", [128, 4, D], BF16)
    wpb = sb("wpb", [128, 4, D], BF16)
    wo = sb("wo", [128, 8, D], BF16)
    E = sb("E", [128, 9, 1024], BF16)
    KBT = sb("KBT", [128, NMAX], BF16)
    VB = sb("VB", [128, NMAX // 128, 2, 66], BF16)
    KAT = sb("KAT", [128, 4, RING * 128], BF16)
    VAr = sb("VAr", [128, RING, 8, 66], BF16)
    hidT = sb("hidT", [128, 4, 8, 128], BF16)
    xr = sb("xr", [128, 2, D], F32)
    bg = sb("bg", [128, 2 * D], BF16)
    tokbf = sb("tokbf", [128, D], BF16)
    featbf = sb("featbf", [128, 8, 128], BF16)
    t1 = sb("t1", [128, 512], F32)
    t2 = sb("t2", [128, 512], F32)
    t3 = sb("t3", [128, 512], F32)
    qAT = sb("qAT", [128, 4, 128], BF16)
    qBT = sb("qBT", [128, 2, 4, 128], BF16)
    ogA = sb("ogA", [128, 2, 512], BF16)
    poS = sb("poS", [128, 2, 512], F32)
    PTA = sb("PTA", [128, 2, 512], BF16)
    PT = sb("PTG", [128, 2, 512], BF16)
    ropeS = sb("ropeS", [128, 128], F32)
    identF = sb("identF", [128, 128], F32)
    identB = sb("identB", [128, 128], BF16)
    qg_bc = sb("qg_bc", [128, 64], F32)
    kg_bc = sb("kg_bc", [128, 64], F32)
    modT = sb("modT", [128, 24, NSEQ], F32)
    Gm = sb("Gm", [128, 8, NSEQ], F32)
    cT = sb("cT_s", [128, 8 * NSEQ], F32)
    scT = sb("scT", [128, 8 * NSEQ], F32)
    badaT = sb("badaT_s", [128, depth * 24], F32)
    ngT = sb("ngT_s", [128, depth * 8], F32)
    ssq = sb("ssq", [128, 1], F32)
    rstd = sb("rstd", [128, 1], F32)
    ssq2 = sb("ssq2", [128, 4], F32)
    rstd2 = sb("rstd2", [128, 4], F32)
    st = sb("st", [128, 8], F32)
    st2 = sb("st2", [128, 8], F32)
    rec = sb("rec", [128, 8], F32)
    recY = sb("recY", [128, 8], F32)
    epsb = sb("epsb", [128, 1], F32)
    oneb = sb("oneb", [128, 1], F32)

    def psb(n, s, d):
        return es.enter_context(nc.psum_tensor(n, s, d))

    P_M = psb("P_M", [128, 512], F32)
    P_A = psb("P_A", [128, 512], F32)
    P_S = [psb(f"P_S{i}", [128, 512], F32) for i in range(2)]
    P_N = psb("P_N", [128, 512], F32)
    P_O = [psb(f"P_O{i}", [128, 512], F32) for i in range(2)]
    P_B = psb("P_B", [128, 512], F32)

    B = {}

    def b(name):
        if name not in B:
            B[name] = Buf(name)
        return B[name]

    state = {"xr": 0, "ps": 0, "pt": 0, "pta": 0, "cast": 0, "xs": 0}

    def next_xr():
        s = state["xr"]
        state["xr"] = (s + 1) % 2
        return s

    fw.dma(sp, lambda: nc.sync.dma_start(out=identF[:], in_=ident_d[:, :]), "c0", writes=[b("identF")])
    fw.op(dve, lambda: V.tensor_copy(out=identB[:], in_=identF[:]), reads=[b("identF")], writes=[b("identB")])
    fw.op(dve, lambda: V.memset(epsb[:], EPS), writes=[b("epsb")])
    fw.op(dve, lambda: V.memset(oneb[:], 1.0), writes=[b("oneb")])
    fw.op(dve, lambda: V.memset(VB[:, :, :, 64:66], 1.0), writes=[b("VBones")])
    fw.op(dve, lambda: V.memset(VAr[:, :, :, 64:66], 1.0), writes=[b("VAones")])
    fw.dma(sp, lambda: nc.sync.dma_start(out=cT[:], in_=cT_d[:, :]), "c1", writes=[b("cT")])
    fw.dma(sp, lambda: nc.sync.dma_start(out=badaT[:], in_=badaT_d[:, :]), "c2", writes=[b("badaT")])
    fw.dma(sp, lambda: nc.sync.dma_start(out=ngT[:], in_=ngT_d[:, :]), "c3", writes=[b("ngT")])
    fw.op(act, lambda: A.activation(out=scT[:], in_=cT[:], func=AF.Exp, scale=-1.0), reads=[b("cT")], writes=[b("scT")])
    fw.op(dve, lambda: V.tensor_scalar_add(out=scT[:], in0=scT[:], scalar1=1.0), reads=[b("scT")], writes=[b("scT")])
    fw.op(dve, lambda: V.reciprocal(out=scT[:], in_=scT[:]), reads=[b("scT")], writes=[b("scT")])
    fw.op(dve, lambda: V.tensor_mul(out=scT[:], in0=scT[:], in1=cT[:]), reads=[b("scT"), b("cT")], writes=[b("scT")])

    def xbuf(s):
        return b(f"xr{s}")

    def cast_op(out_ap, in_ap, reads, writes):
        k = state["cast"]
        state["cast"] = k + 1
        if k % 2 == 0:
            fw.op(dve, lambda: V.tensor_copy(out=out_ap, in_=in_ap), reads=reads, writes=writes)
        else:
            fw.op(pool, lambda: G.tensor_copy(out=out_ap, in_=in_ap), reads=reads, writes=writes)

    def load_cast(dst_ap, src_ap, ncols, wbuf):
        s = next_xr()
        fw.dma(sp, lambda: nc.sync.dma_start(out=xr[:, s, 0:ncols], in_=src_ap), f"xr{s}", writes=[xbuf(s)])
        cast_op(dst_ap, xr[:, s, 0:ncols], [xbuf(s)], [wbuf])

    def headnorm_rope(src_ps, src_buf, H, gbc, gbuf, dst4, dst_bufs, perm, alt=None):
        W = H * 64
        s3 = src_ps.rearrange("p (h d) -> p h d", d=64)
        if alt is None:
            f1, f2, f3 = t1[:, 0:W], t2[:, 0:W], t3[:, 0:W]
            B1, B2, B3 = b("t1"), b("t2"), b("t3")
            st_, st2_, stb, st2b = st[:, 0:H], st2[:, 0:H], b("st"), b("st2")
            rp, rpb_ = ropeS, b("ropeS")
        else:
            tt_, ttb, c0, rp, rpb_ = alt
            f1, f2, f3 = tt_[:, 0:W], tt_[:, 128:128 + W], tt_[:, 256:256 + W]
            B1 = B2 = B3 = ttb
            st_, st2_, stb, st2b = st[:, c0:c0 + H], st2[:, c0:c0 + H], b(f"st{c0}"), b(f"st2{c0}")
        a1 = f1.rearrange("p (h d) -> p h d", d=64)
        a2 = f2.rearrange("p (h d) -> p h d", d=64)
        a3 = f3.rearrange("p (h d) -> p h d", d=64)
        fw.op(act, lambda: A.activation(out=a1, in_=s3, func=AF.Square), reads=[src_buf], writes=[B1])
        fw.op(dve, lambda: V.tensor_reduce(out=st_, in_=a1, axis=AX.X, op=ALU.add), reads=[B1], writes=[stb])
        fw.op(act, lambda: A.activation(out=st2_, in_=st_, func=AF.Ln, scale=1.0 / 64, bias=epsb[:, 0:1]),
              reads=[stb, b("epsb")], writes=[st2b])
        fw.op(act, lambda: A.activation(out=st2_, in_=st2_, func=AF.Exp, scale=-0.5), reads=[st2b], writes=[st2b])
        fw.op(dve, lambda: V.tensor_mul(out=a2, in0=s3, in1=st2_.unsqueeze(2).broadcast_to([128, H, 64])),
              reads=[src_buf, st2b], writes=[B2])
        fw.op(dve, lambda: V.tensor_mul(out=a2, in0=a2, in1=gbc[:].unsqueeze(1).broadcast_to([128, H, 64])),
              reads=[B2, gbuf], writes=[B2])
        fw.op(dve, lambda: V.tensor_mul(out=a1, in0=a2, in1=rp[:, 0:64].unsqueeze(1).broadcast_to([128, H, 64])),
              reads=[B2, rpb_], writes=[B1])
        fns = []
        for g in range(2):
            for f in range(2):
                o = g * 32 + f * 16
                i = g * 32 + (1 - f) * 16
                fns.append(lambda o=o, i=i: V.tensor_mul(out=a3[:, :, o:o + 16], in0=a2[:, :, i:i + 16],
                                                         in1=rp[:, 64 + o:64 + o + 16].unsqueeze(1).broadcast_to([128, H, 16])))
        fw.op(dve, fns, reads=[B2, rpb_], writes=[B3])
        if perm:
            i1 = f1.rearrange("p (k g d) -> p k g d", k=2, g=4)
            i3 = f3.rearrange("p (k g d) -> p k g d", k=2, g=4)
        else:
            i1, i3 = a1, a3
        fw.op(dve, lambda: V.tensor_add(out=dst4, in0=i1, in1=i3), reads=[B1, B3], writes=dst_bufs)

    def bfv(pt):
        return pt[:].bitcast(BF16)

    def norm_front(xs, s, hs, pt, ptb, alt=None):
        if isinstance(xs, tuple):
            xa, xbl = xs
        else:
            xa, xbl = xr[:, xs, :], [xbuf(xs)]
        pv = bfv(pt)
        if alt is None:
            xn, xnb, ssq_, ssqb, rstd_, rstdb = tokbf[:], b("tokbf"), ssq[:], b("ssq"), rstd[:], b("rstd")
        else:
            xn, xnb, ssq_, ssqb, rstd_, rstdb = alt
        fw.op(dve, lambda: V.memset(ssq_, 0.0), writes=[ssqb])
        fw.op(dve, lambda: V.scalar_tensor_tensor(out=xn, in0=xa, scalar=1.0, in1=xa, op0=ALU.mult, op1=ALU.mult,
                                                  accum_out=ssq_), reads=xbl + [ssqb], writes=[xnb, ssqb])
        fw.op(act, lambda: A.activation(out=rstd_, in_=ssq_, func=AF.Ln, scale=1.0 / D, bias=epsb[:, 0:1]),
              reads=[ssqb, b("epsb")], writes=[rstdb])
        fw.op(act, lambda: A.activation(out=rstd_, in_=rstd_, func=AF.Exp, scale=-0.5), reads=[rstdb], writes=[rstdb])
        fw.op(dve, lambda: V.tensor_scalar_mul(out=xn, in0=xa, scalar1=rstd_), reads=xbl + [rstdb], writes=[xnb])
        fw.op(pe, [lambda j=j: T.transpose(pv[:, j * 128:(j + 1) * 128], xn[:, j * 128:(j + 1) * 128], identB[:]) for j in range(8)],
              reads=[xnb, b("identB")], writes=[ptb])
        fw.op(act, [lambda j=j: A.activation(out=hidT[:, hs, j, :], in_=pv[:, j * 128:(j + 1) * 128], func=AF.Identity,
                                             scale=Gm[:, j, s:s + 1], bias=modT[:, j, s:s + 1]) for j in range(8)],
              reads=[ptb, b("Gm"), b("modT")], writes=[b(f"hidT{hs}")])

    def proj_tm(ps_ap, ps_buf, hs, col0, ncols):
        fw.op(pe, [lambda j=j: T.matmul(ps_ap, lhsT=hidT[:, hs, j, :], rhs=wi[:, j, col0:col0 + ncols], start=(j == 0), stop=(j == 7))
                   for j in range(8)], reads=[b(f"hidT{hs}"), b("wi")], writes=[ps_buf])

    def proj_fm(ps_t, ps_buf, hs, col0, nch):
        fns = []
        for c in range(nch):
            for j in range(8):
                fns.append(lambda c=c, j=j: T.matmul(ps_t[:, c * 128:(c + 1) * 128], lhsT=wi[:, j, col0 + c * 128:col0 + (c + 1) * 128],
                                                      rhs=hidT[:, hs, j, :], start=(j == 0), stop=(j == 7)))
        fw.op(pe, fns, reads=[b(f"hidT{hs}"), b("wi")], writes=[ps_buf])

    def denom_from(ps_ap, ps_buf, tt, tbuf, bias_ap=None, bias_buf=None):
        if bias_ap is not None:
            fw.op(dve, lambda: V.tensor_add(out=tt, in0=ps_ap, in1=bias_ap), reads=[ps_buf, bias_buf], writes=[tbuf])
            fw.op(act, lambda: A.activation(out=tt, in_=tt, func=AF.Exp, scale=-1.0), reads=[tbuf], writes=[tbuf])
        else:
            fw.op(act, lambda: A.activation(out=tt, in_=ps_ap, func=AF.Exp, scale=-1.0), reads=[ps_buf], writes=[tbuf])
        fw.op(act, lambda: A.activation(out=tt, in_=tt, func=AF.Ln, scale=1.0, bias=oneb[:, 0:1]), reads=[tbuf, b("oneb")], writes=[tbuf])
        fw.op(act, lambda: A.activation(out=tt, in_=tt, func=AF.Exp, scale=-1.0), reads=[tbuf], writes=[tbuf])

    def merge2(a, c):
        out, ia, ic = [], 0, 0
        na, ncq = len(a), len(c)
        while ia < na or ic < ncq:
            if ic >= ncq or (ia < na and ia * ncq <= ic * na):
                out.append(a[ia])
                ia += 1
            else:
                out.append(c[ic])
                ic += 1
        return out

    def merge_n(qs):
        out = []
        pos = [0] * len(qs)
        tot = sum(len(q) for q in qs)
        while len(out) < tot:
            best, bi = None, -1
            for i, q in enumerate(qs):
                if pos[i] < len(q):
                    frac = pos[i] / len(q)
                    if best is None or frac < best:
                        best, bi = frac, i
            out.append(qs[bi][pos[bi]])
            pos[bi] += 1
        return out

    def chk2(n):
        if STOP is not None and n == STOP:
            fw.finish()
            raise _Stop(nc, fw)

    chk2(0)
    for l in range(depth):
        dst = yout if (depth - 1 - l) % 2 == 0 else scr
        dname = "yout" if dst is yout else "scr"
        if l == 0:
            src, sname = xin, "xin"
        else:
            src = yout if (depth - l) % 2 == 0 else scr
            sname = "yout" if src is yout else "scr"

        for fidx in range(24):
            s = next_xr()
            fw.dma(sp, lambda s=s, fidx=fidx: nc.sync.dma_start(
                out=xr[:, s, :].rearrange("p (k c) -> p k c", c=128),
                in_=wada_d[l, :, fidx * 128:(fidx + 1) * 128].rearrange("(k p) c -> p k c", p=128)), f"xr{s}", writes=[xbuf(s)])
            fw.op(pe, [lambda k=k, s=s, fidx=fidx: T.matmul(P_A[:, fidx * NSEQ:(fidx + 1) * NSEQ],
                                                          lhsT=xr[:, s, k * 128:(k + 1) * 128], rhs=scT[:, k * NSEQ:(k + 1) * NSEQ],
                                                          start=(k == 0), stop=(k == 7)) for k in range(8)],
                  reads=[xbuf(s), b("scT")], writes=[b("P_A")])
        fw.op(dve, lambda: V.tensor_add(out=modT[:], in0=P_A[:, 0:24 * NSEQ].rearrange("p (f s) -> p f s", s=NSEQ),
                                        in1=badaT[:, l * 24:(l + 1) * 24].unsqueeze(2).broadcast_to([128, 24, NSEQ])),
              reads=[b("P_A"), b("badaT")], writes=[b("modT")])
        fw.op(dve, lambda: V.tensor_scalar_add(out=Gm[:], in0=modT[:, 8:16, :], scalar1=1.0), reads=[b("modT")], writes=[b("Gm")])
        fw.op(dve, lambda: V.tensor_mul(out=Gm[:], in0=Gm[:], in1=ngT[:, l * 8:(l + 1) * 8].unsqueeze(2).broadcast_to([128, 8, NSEQ])),
              reads=[b("Gm"), b("ngT")], writes=[b("Gm")])

        chk2(1)
        pieces = [(0, 1024), (1024, 1024), (2048, 1024), (3072, 1024), (4096, 1024), (5120, 256)]
        for k in range(8):
            for (c0, n) in pieces:
                load_cast(wi[:, k, c0:c0 + n], win_d[l, k * 128:(k + 1) * 128, c0:c0 + n], n, b("wi"))
        for k in range(4):
            load_cast(wpa[:, k, :], wpa_d[l, k * 128:(k + 1) * 128, :], D, b("wpa"))
            load_cast(wpb[:, k, :], wpb_d[l, k * 128:(k + 1) * 128, :], D, b("wpb"))
        for h2 in range(2):
            load_cast(bg[:, h2 * D:(h2 + 1) * D], bgate_d[l, h2 * D:(h2 + 1) * D].partition_broadcast(128), D, b("bg"))
        fw.dma(sp, lambda: nc.sync.dma_start(out=qg_bc[:], in_=qg_d[l, :].partition_broadcast(128)), "c4", writes=[b("qg")])
        fw.dma(sp, lambda: nc.sync.dma_start(out=kg_bc[:], in_=kg_d[l, :].partition_broadcast(128)), "c5", writes=[b("kg")])
        chk2(2)
        for t9 in range(9):
            s = next_xr()
            fw.dma(sp, lambda s=s, t9=t9: nc.sync.dma_start(out=xr[:, s, :], in_=rpbg_d[l, t9, :, :]), f"xr{s}", writes=[xbuf(s)])
            fw.op(act, lambda s=s, t9=t9: A.activation(out=E[:, t9, :], in_=xr[:, s, :], func=AF.Exp), reads=[xbuf(s)], writes=[b("E")])

        chk2(3)
        off = 0
        for s_i, N in enumerate(seq_lens):
            NT = N // 128
            sq = s_i

            def dbuf(name, t):
                return b(f"{name}:{off // 128 + t}")

            def load_x(t, which, nm):
                s = next_xr()
                r0 = off + t * 128
                fw.dma(sp, lambda: nc.sync.dma_start(out=xr[:, s, :], in_=which[r0:r0 + 128, :]), f"xr{s}",
                       reads=[dbuf(nm, t)], writes=[xbuf(s)])
                return s

            def load_rope(t):
                fw.dma(sp, lambda: nc.sync.dma_start(out=ropeS[:], in_=rope_d[t * 128:(t + 1) * 128, :]), "rope", writes=[b("ropeS")])

            for half in range(2):
                pst, pbuf = (P_A, b("P_A")) if half == 0 else (P_B, b("P_B"))
                tg, tgb = (t1, b("t1")) if half == 0 else (t2, b("t2"))
                for jj in range(4):
                    j = half * 4 + jj
                    fw.op(dve, lambda: V.memset(t3[:, 0:128], 1.0), writes=[b("t3")])
                    fw.op(dve, lambda j=j: V.tensor_scalar_mul(out=t3[:, 128:256], in0=identF[:], scalar1=modT[:, 16 + j, sq:sq + 1]),
                          reads=[b("identF"), b("modT")], writes=[b("t3")])
                    fw.op(pe, lambda jj=jj, pst=pst: T.matmul(pst[:, jj * 128:(jj + 1) * 128], lhsT=t3[:, 0:128], rhs=t3[:, 128:256], start=True, stop=True),
                          reads=[b("t3")], writes=[pbuf])
                fw.op(dve, lambda pst=pst, tg=tg: V.tensor_copy(out=tg[:], in_=pst[:]), reads=[pbuf], writes=[tgb])
            for k in range(8):
                s = next_xr()
                fw.dma(sp, lambda s=s, k=k: nc.sync.dma_start(out=xr[:, s, :], in_=wout_d[l, k * 128:(k + 1) * 128, :]), f"xr{s}", writes=[xbuf(s)])
                fw.op(dve, lambda s=s, k=k: V.tensor_mul(out=wo[:, k, 0:512], in0=xr[:, s, 0:512], in1=t1[:]), reads=[xbuf(s), b("t1")], writes=[b("wo")])
                fw.op(pool, lambda s=s, k=k: G.tensor_mul(out=wo[:, k, 512:1024], in0=xr[:, s, 512:1024], in1=t2[:]), reads=[xbuf(s), b("t2")], writes=[b("wo")])

            chk2(4)
            fbv = featbf[:].rearrange("p c t -> p (c t)")
            ogv = ogA[:].rearrange("p a t -> p (a t)")
            qbv = qBT[:].rearrange("p a c t -> p (a c t)")
            NCH = 3
            katf = KAT[:].rearrange("p c t -> p (c t)")[:, 0:2048].bitcast(F32)
            katb = [b(f"KAT{r_}") for r_ in range(RING)]
            xslots = [(xr[:, 0, :], [xbuf(0)], "xr0"), (xr[:, 1, :], [xbuf(1)], "xr1"), (katf, katb, "xk")]
            sets = [
                (P_M, "P_M", P_A, "P_A", tokbf[:], "tokbf", t1[:], "t1", ropeS[:], "ropeS"),
                (P_N, "P_N", P_B, "P_B", fbv, "featbf", t2[:], "t2", t3[:, 0:128], "t3a"),
                (P_S[0], "P_S0", P_O[0], "P_O0", ogv, "ogAall", poS[:, 0, :], "poS0", t3[:, 128:256], "t3b"),
                (P_S[1], "P_S1", P_O[1], "P_O1", qbv, "qBTall", poS[:, 1, :], "poS1", t3[:, 256:384], "t3c"),
            ]

            def pass1_tile(t):
                e = t % NCH
                pt, ptn, pj, pjn, xnv, xnn, hsc, hscn, rpt, rpn = sets[e]
                ptb, pjb, xnb = b(ptn), b(pjn), b(xnn)
                xap, xbl, xkey = xslots[e]
                r0 = off + t * 128
                fw.dma(sp, lambda: nc.sync.dma_start(out=xap, in_=src[r0:r0 + 128, :]), xkey, reads=[dbuf(sname, t)], writes=xbl)
                xs = (xap, xbl)
                hs = e
                fw.dma(sp, lambda: nc.sync.dma_start(out=rpt, in_=rope_d[t * 128:(t + 1) * 128, :]), f"rope{e}", writes=[b(rpn)])
                nalt = (xnv, xnb, ssq2[:, e:e + 1], b(f"ssq2{e}"), rstd2[:, e:e + 1], b(f"rstd2{e}"))
                norm_front(xs, sq, hs, pt, ptb, nalt)
                proj_tm(pj[:, 0:256], pjb, hs, KB, 256)
                fw.op(act, lambda: A.copy(out=VB[:, t, :, 0:64], in_=pj[:, 128:256].rearrange("p (h d) -> p h d", d=64)),
                      reads=[pjb], writes=[b("VB")])
                kst = xnv[:, 0:128]
                headnorm_rope(pj[:, 0:128], pjb, 2, kg_bc, b("kg"), kst.rearrange("p (h d) -> p h d", d=64), [xnb], False,
                              alt=(hsc, b(hscn), 2 * e, rpt, b(rpn)))
                fw.op(pe, lambda: T.transpose(bfv(pt)[:, 0:128], kst, identB[:]), reads=[xnb, b("identB")], writes=[ptb])
                fw.op(dve, lambda: V.tensor_copy(out=KBT[:, t * 128:(t + 1) * 128], in_=bfv(pt)[:, 0:128]), reads=[ptb], writes=[b(f"KBT{e}")])

            qs = [[] for _ in range(NCH)]
            for t in range(NT):
                fw.begin_defer()
                pass1_tile(t)
                qs[t % NCH].extend(fw.end_defer())
            for th in merge_n(qs):
                th()

            chk2(5)
            ya1 = poS[:, 0, :]
            ya2 = poS[:, 1, :]

            def FE(t, alt2=False):
                xs = load_x(t, src, sname)
                hs = t % 4
                if alt2:
                    pa, pab, pbk, pbb = P_N, b("P_N"), P_M, b("P_M")
                    nalt = (fbv, b("featbf"), ssq2[:, 0:1], b("ssq20"), rstd2[:, 0:1], b("rstd20"))
                else:
                    pa, pab, pbk, pbb = P_A, b("P_A"), P_B, b("P_B")
                    nalt = None
                norm_front(xs, sq, hs, pa, pab, nalt)
                rs = t % RING
                proj_fm(pbk, pbb, hs, KA, 4)
                fw.op(dve, lambda: V.tensor_copy(out=KAT[:, :, rs * 128:(rs + 1) * 128], in_=pbk[:].rearrange("p (c t) -> p c t", t=128)),
                      reads=[pbb], writes=[b(f"KAT{rs}")])
                proj_tm(pa[:, 0:512], pab, hs, VA, 512)
                fw.op(dve, lambda: V.tensor_copy(out=VAr[:, rs, :, 0:64], in_=pa[:].rearrange("p (h d) -> p h d", d=64)),
                      reads=[pab], writes=[b(f"VAr{rs}")])

            def normalize_part(hg, pt_, pbuf_, strided, rc, rcb, o2, o2b):
                fw.op(dve, lambda: V.reciprocal(out=rc[:, hg * 4:(hg + 1) * 4].unsqueeze(2),
                                                in_=pt_[:, 0:264].rearrange("p (h e) -> p h e", e=66)[:, :, 64:65]),
                      reads=[pbuf_], writes=[rcb])
                fw.op(dve, lambda: V.tensor_mul(
                    out=(o2.rearrange("p (c two d) -> p two c d", two=2, d=64)[:, hg] if strided
                         else o2[:, hg * 256:(hg + 1) * 256].rearrange("p (h d) -> p h d", d=64)),
                    in0=pt_[:, 0:264].rearrange("p (h e) -> p h e", e=66)[:, :, 0:64],
                    in1=rc[:, hg * 4:(hg + 1) * 4].unsqueeze(2).broadcast_to([128, 4, 64])),
                    reads=[pbuf_, rcb], writes=[o2b])

            def zgate(hs, zcol, zbank, zbuf, s1, s1b, o2, o2b, out_ap, out_buf):
                proj_tm(zbank[:, 0:512], zbuf, hs, zcol, 512)
                denom_from(zbank[:, 0:512], zbuf, s1, s1b)
                fw.op(dve, lambda: V.tensor_mul(out=s1, in0=zbank[:, 0:512], in1=s1), reads=[s1b, zbuf], writes=[s1b])
                fw.op(dve, lambda: V.tensor_mul(out=out_ap, in0=o2, in1=s1), reads=[s1b, o2b], writes=[out_buf])

            def Z1(t):
                hs = t % 4
                tb = t % 2
                proj_fm(P_N, b("P_N"), hs, QA, 4)
                fw.op(dve, lambda: V.tensor_copy(out=qAT[:], in_=P_N[:].rearrange("p (c t) -> p c t", t=128)), reads=[b("P_N")], writes=[b("qAT")])
                load_rope(t)
                proj_tm(P_M[:, 0:512], b("P_M"), hs, QB, 512)
                headnorm_rope(P_M[:, 0:512], b("P_M"), 8, qg_bc, b("qg"),
                              PTA[:, 0, :].rearrange("p (g k d) -> p k g d", g=4, k=2), [b("PTA0")], True)
                pvn = bfv(P_N)
                fw.op(pe, [lambda g=g: T.transpose(pvn[:, g * 128:(g + 1) * 128], PTA[:, 0, g * 128:(g + 1) * 128], identB[:]) for g in range(4)],
                      reads=[b("PTA0"), b("identB")], writes=[b("P_N")])
                fw.op(dve, lambda: V.tensor_copy(out=qBT[:, tb], in_=pvn[:, 0:512].rearrange("p (c t) -> p c t", t=128)),
                      reads=[b("P_N")], writes=[b(f"qBT{tb}")])

            def Z2(t):
                hs = t % 4
                tb = t % 2
                kts, tabs = na_plan(t, NT)
                nk = len(kts)
                items = [(par, idx, kt, tab) for par in range(2) for idx, (kt, tab) in enumerate(zip(kts, tabs))]

                def NA_S(n):
                    par, idx, kt, tab = items[n]
                    pb = par * 64
                    rs = kt % RING
                    fw.op(pe, [lambda hh=hh: T.matmul(
                        P_M[:, hh * 128:(hh + 1) * 128], lhsT=KAT[pb:pb + 64, hh, rs * 128:(rs + 1) * 128],
                        rhs=qAT[pb:pb + 64, hh, :], start=True, stop=True) for hh in range(4)],
                        reads=[b(f"KAT{rs}"), b("qAT")], writes=[b("P_M")])

                def NA_exp(n):
                    sl = n % 2
                    fw.op(act, lambda: A.activation(out=PTA[:, sl, :], in_=P_M[:], func=AF.Exp, scale=0.125),
                          reads=[b("P_M")], writes=[b(f"PTA{sl}")])

                def NA_rest(n):
                    par, idx, kt, tab = items[n]
                    rs = kt % RING
                    sl = n % 2
                    fw.op(dve, lambda: V.tensor_mul(
                        out=PTA[:, sl, :].rearrange("p (c q) -> p c q", q=128), in0=PTA[:, sl, :].rearrange("p (c q) -> p c q", q=128),
                        in1=E[:, tab, :].rearrange("p (c two q) -> p two c q", two=2, q=128)[:, par]),
                        reads=[b(f"PTA{sl}"), b("E")], writes=[b(f"PTA{sl}")])
                    fw.op(pe, [lambda hh=hh: T.matmul(
                        P_N[:, hh * 66:(hh + 1) * 66], lhsT=PTA[:, sl, hh * 128:(hh + 1) * 128], rhs=VAr[:, rs, 2 * hh + par, :],
                        start=(idx == 0 and hh == 0), stop=(idx == nk - 1 and hh == 3)) for hh in range(4)],
                        reads=[b(f"PTA{sl}"), b(f"VAr{rs}"), b("VAones")], writes=[b("P_N")])
                    if idx == nk - 1:
                        normalize_part(par, P_N, b("P_N"), True, rec, b("rec"), t2[:], b("t2"))

                NA_S(0)
                for n in range(len(items)):
                    NA_exp(n)
                    if n + 1 < len(items):
                        NA_S(n + 1)
                    NA_rest(n)
                zgate(hs, ZA, P_M, b("P_M"), t1[:], b("t1"), t2[:], b("t2"), ogA[:, tb, :], b(f"ogA{tb}"))

            def P567(t):
                hs = t % 4
                tb = t % 2
                for kvh in range(2):
                    ptr, ptrb = (P_A, b("P_A")) if kvh == 0 else (P_B, b("P_B"))
                    fw.op(pe, [lambda g=g, ptr=ptr, kvh=kvh: T.transpose(ptr[:, g * 66:(g + 1) * 66], poS[0:66, kvh, g * 128:(g + 1) * 128], identF[0:66, 0:66])
                               for g in range(4)], reads=[b(f"poS{kvh}"), b("identF")], writes=[ptrb])
                normalize_part(0, P_A, b("P_A"), False, recY, b("recY"), ya2, b("poS1"))
                normalize_part(1, P_B, b("P_B"), False, recY, b("recY"), ya2, b("poS1"))
                zgate(hs, ZB, P_A, b("P_A"), ya1, b("poS0"), ya2, b("poS1"), tokbf[:, 512:1024], b("tokbf"))
                pvb = bfv(P_B)
                fw.op(pe, [lambda c=c: T.transpose(pvb[:, c * 128:(c + 1) * 128], ogA[:, tb, c * 128:(c + 1) * 128], identB[:]) for c in range(4)] +
                          [lambda c=c: T.transpose(pvb[:, c * 128:(c + 1) * 128], tokbf[:, c * 128:(c + 1) * 128], identB[:]) for c in range(4, 8)],
                      reads=[b(f"ogA{tb}"), b("tokbf"), b("identB")], writes=[b("P_B")])
                fw.op(dve, lambda: V.tensor_copy(out=featbf[:], in_=pvb.rearrange("p (c t) -> p c t", t=128)), reads=[b("P_B")], writes=[b("featbf")])
                for half in range(2):
                    cs = slice(half * 512, (half + 1) * 512)
                    proj_tm(P_A[:, 0:512], b("P_A"), hs, GA + half * 512, 512)
                    fw.op(pe, [lambda c=c, cs=cs: T.matmul(P_B[:, 0:512], lhsT=featbf[:, c, :], rhs=wpa[:, c, cs], start=(c == 0), stop=(c == 3))
                               for c in range(4)], reads=[b("featbf"), b("wpa")], writes=[b("P_B")])
                    denom_from(P_A[:, 0:512], b("P_A"), ya1, b("poS0"), bg[:, half * 512:(half + 1) * 512], b("bg"))
                    proj_tm(P_A[:, 0:512], b("P_A"), hs, GB + half * 512, 512)
                    fw.op(dve, lambda: V.tensor_mul(out=ya1, in0=P_B[:, 0:512], in1=ya1), reads=[b("P_B"), b("poS0")], writes=[b("poS0")])
                    fw.op(pe, [lambda c=c, cs=cs: T.matmul(P_B[:, 0:512], lhsT=featbf[:, 4 + c, :], rhs=wpb[:, c, cs], start=(c == 0), stop=(c == 3))
                               for c in range(4)], reads=[b("featbf"), b("wpb")], writes=[b("P_B")])
                    denom_from(P_A[:, 0:512], b("P_A"), ya2, b("poS1"), bg[:, D + half * 512:D + (half + 1) * 512], b("bg"))
                    fw.op(dve, lambda: V.tensor_mul(out=ya2, in0=P_B[:, 0:512], in1=ya2), reads=[b("P_B"), b("poS1")], writes=[b("poS1")])
                    fw.op(dve, lambda cs=cs: V.tensor_add(out=tokbf[:, cs], in0=ya1, in1=ya2), reads=[b("poS0"), b("poS1")], writes=[b("tokbf")])
                pva = bfv(P_A)
                fw.op(pe, [lambda c=c: T.transpose(pva[:, c * 128:(c + 1) * 128], tokbf[:, c * 128:(c + 1) * 128], identB[:]) for c in range(8)],
                      reads=[b("tokbf"), b("identB")], writes=[b("P_A")])
                fw.op(dve, lambda: V.tensor_copy(out=featbf[:], in_=pva.rearrange("p (c t) -> p c t", t=128)), reads=[b("P_A")], writes=[b("featbf")])
                xs = load_x(t, src, sname)
                for half in range(2):
                    cs = slice(half * 512, (half + 1) * 512)
                    pst, pbuf = (P_B, b("P_B")) if half == 0 else (P_A, b("P_A"))
                    fw.op(pe, [lambda j=j, cs=cs, pst=pst: T.matmul(pst[:, 0:512], lhsT=featbf[:, j, :], rhs=wo[:, j, cs], start=(j == 0), stop=(j == 7))
                               for j in range(8)], reads=[b("featbf"), b("wo")], writes=[pbuf])
                    fw.op(dve, lambda cs=cs, pst=pst: V.tensor_add(out=xr[:, xs, cs], in0=xr[:, xs, cs], in1=pst[:, 0:512]),
                          reads=[xbuf(xs), pbuf], writes=[xbuf(xs)])
                r0 = off + t * 128
                fw.dma(pool, lambda: nc.gpsimd.dma_start(out=dst[r0:r0 + 128, :], in_=xr[:, xs, :]), f"st{xs}",
                       reads=[xbuf(xs)], writes=[dbuf(dname, t)])

            def GQA(t):
                tb = t % 2
                for kvh in range(2):
                    pb = kvh * 64
                    po, pob = P_O[kvh], b(f"P_O{kvh}")

                    def S(kt, pb=pb):
                        pi = kt % 2
                        fw.op(pe, lambda: T.matmul(P_S[pi][:, 0:512], lhsT=KBT[pb:pb + 64, kt * 128:(kt + 1) * 128],
                                                   rhs=qBT[pb:pb + 64, tb].rearrange("p c t -> p (c t)"), start=True, stop=True),
                              reads=[b("KBT0"), b("KBT1"), b("KBT2"), b(f"qBT{tb}")], writes=[b(f"P_S{pi}")])

                    S(0)
                    for kt in range(NT):
                        if kt + 1 < NT:
                            S(kt + 1)
                        pi = kt % 2
                        fw.op(act, lambda pi=pi: A.activation(out=PT[:, pi, :], in_=P_S[pi][:], func=AF.Exp, scale=0.125),
                              reads=[b(f"P_S{pi}")], writes=[b(f"PT{pi}")])
                        fw.op(pe, lambda pi=pi, kt=kt, po=po, kvh=kvh: T.matmul(po[0:66, 0:512], lhsT=VB[:, kt, kvh, :], rhs=PT[:, pi, :],
                                                                start=(kt == 0), stop=(kt == NT - 1)),
                              reads=[b(f"PT{pi}"), b("VB"), b("VBones")], writes=[pob])

            def run_merged(gq, xq):
                for th in merge2(gq, xq):
                    th()

            fw.begin_defer()
            for t in range(0, min(4, NT), 2):
                state["xr"] = 0
                FE(t)
            qa_ = fw.end_defer()
            fw.begin_defer()
            for t in range(1, min(4, NT), 2):
                state["xr"] = 1
                FE(t, alt2=True)
            qb_ = fw.end_defer()
            for th in merge2(qa_, qb_):
                th()
            Z1(0)
            chk2(6)
            for t in range(NT):
                def dq(*calls):
                    fw.begin_defer()
                    for c in calls:
                        c()
                    return fw.end_defer()
                y1 = dq(*([lambda: P567(t - 1)] if t >= 1 else []))
                z2 = dq(lambda: Z2(t))
                y2 = dq(*([lambda: FE(t + 3)] if (t >= 1 and t + 3 < NT) else []))
                z1 = dq(*([lambda: Z1(t + 1)] if t + 1 < NT else []))
                gq = dq(lambda: GQA(t))
                run_merged(gq, merge2(y1, z2) + merge2(y2, z1))
                for kvh in range(2):
                    fw.op(act, lambda kvh=kvh: A.copy(out=poS[0:66, kvh, :], in_=P_O[kvh][0:66, :]), reads=[b(f"P_O{kvh}")], writes=[b(f"poS{kvh}")])
            P567(NT - 1)
            chk2(11)
            off += N

    fgb = E[:, 0:2, :].rearrange("p a b -> p (a b)").bitcast(F32)
    fw.dma(sp, lambda: nc.sync.dma_start(out=fgb, in_=fg_d.partition_broadcast(128)), "c6", writes=[b("E")])
    fjunk = [(tokbf[:], "tokbf"), (featbf[:].rearrange("p c t -> p (c t)"), "featbf")]
    for tt in range(TOT // 128):
        s = next_xr()
        r0 = tt * 128
        e = tt % 2
        ssq_, ssqb = ssq2[:, e:e + 1], b(f"fssq{e}")
        rstd_, rstdb = rstd2[:, e:e + 1], b(f"frstd{e}")
        jk, jkn = fjunk[e]
        fw.dma(sp, lambda s=s, r0=r0: nc.sync.dma_start(out=xr[:, s, :], in_=yout[r0:r0 + 128, :]), f"xr{s}", reads=[b(f"yout:{tt}")], writes=[xbuf(s)])
        xa = xr[:, s, :]
        fw.op(dve, lambda ssq_=ssq_: V.memset(ssq_, 0.0), writes=[ssqb])
        fw.op(dve, lambda xa=xa, jk=jk, ssq_=ssq_: V.scalar_tensor_tensor(out=jk, in0=xa, scalar=1.0, in1=xa, op0=ALU.mult, op1=ALU.mult, accum_out=ssq_),
              reads=[xbuf(s), ssqb], writes=[b(jkn), ssqb])
        fw.op(act, lambda ssq_=ssq_, rstd_=rstd_: A.activation(out=rstd_, in_=ssq_, func=AF.Ln, scale=1.0 / D, bias=epsb[:, 0:1]), reads=[ssqb, b("epsb")], writes=[rstdb])
        fw.op(act, lambda rstd_=rstd_: A.activation(out=rstd_, in_=rstd_, func=AF.Exp, scale=-0.5), reads=[rstdb], writes=[rstdb])
        fw.op(dve, lambda xa=xa, rstd_=rstd_: V.scalar_tensor_tensor(out=xa, in0=xa, scalar=rstd_, in1=fgb, op0=ALU.mult, op1=ALU.mult),
              reads=[xbuf(s), rstdb, b("E")], writes=[xbuf(s)])
        fw.dma(pool, lambda s=s, r0=r0: nc.gpsimd.dma_start(out=yout[r0:r0 + 128, :], in_=xr[:, s, :]), f"st{s}", reads=[xbuf(s)], writes=[b(f"yout:{tt}")])
    fw.finish()
    return nc, fw


def _const_tables(nmax):
    t = np.arange(nmax)
    row = (t // 64).astype(np.float32)
    col = (t % 64).astype(np.float32)
    inv = (10000.0 ** (-np.arange(0, 32, 2, dtype=np.float32) / 32)).astype(np.float32)
    ar = row[:, None] * inv
    ac = col[:, None] * inv
    C = np.concatenate([np.cos(ar), np.cos(ar), np.cos(ac), np.cos(ac)], axis=1)
    S = np.concatenate([-np.sin(ar), np.sin(ar), -np.sin(ac), np.sin(ac)], axis=1)
    return np.ascontiguousarray(np.concatenate([C, S], axis=1).astype(np.float32))


def _rpb_gather_index():
    kr = np.arange(128) // 64
    kc = np.arange(128) % 64
    qr, qc = kr, kc
    cs = np.clip(qc - 8, 0, 48)
    colok = (kc[:, None] >= cs[None, :]) & (kc[:, None] < cs[None, :] + 16)
    coff = kc[:, None] - qc[None, :] + 15
    idx = np.full((9, 128, 128), 465, dtype=np.int64)
    specs = [(-3, None), (-2, None), (-1, None), (0, None), (1, None), (2, None), (3, None), (-2, "int"), (2, "int")]
    for ti, (dt_, kind) in enumerate(specs):
        dr = 2 * dt_ + kr[:, None] - qr[None, :]
        ok = colok & (dr + 7 >= 0) & (dr + 7 <= 14)
        if kind == "int":
            ok = ok & (dr >= -4) & (dr <= 3)
        flat = (np.clip(dr + 7, 0, 14)) * 31 + np.clip(coff, 0, 30)
        idx[ti] = np.where(ok, flat, 465)
    return idx


def _host_inputs(core_seqs_x, core_c, W, depth, nmax):
    nseq = len(core_seqs_x)
    xin = np.ascontiguousarray(np.concatenate(core_seqs_x, axis=0))
    cmat = np.stack(core_c, axis=0)
    cT = np.ascontiguousarray(cmat.reshape(nseq, 8, 128).transpose(2, 1, 0).reshape(128, 8 * nseq))
    m = dict(W)
    m["xin"] = xin
    m["cT"] = cT
    return m


def _shared_inputs(norm_g, w_ada, b_ada, w_in, b_gate, rpb, q_norm_g, k_norm_g, w_pa, w_pb, w_out, final_g, depth, nmax):
    ngT = np.ascontiguousarray(norm_g[:depth].reshape(depth, 8, 128).transpose(2, 0, 1).reshape(128, depth * 8))
    badaT = np.ascontiguousarray(b_ada[:depth].reshape(depth, 24, 128).transpose(2, 0, 1).reshape(128, depth * 24))
    idx = _rpb_gather_index()
    rp = rpb[:depth].reshape(depth, 8, 465)
    rp = np.concatenate([rp, np.full((depth, 8, 1), NEG, np.float32)], axis=2)
    g = rp[:, :, idx]
    rpbg = np.ascontiguousarray(g.transpose(0, 2, 3, 1, 4).reshape(depth, 9, 128, 1024))
    return {
        "ngT": ngT.astype(np.float32), "badaT": badaT.astype(np.float32),
        "w_ada": np.ascontiguousarray(w_ada[:depth]), "w_in": np.ascontiguousarray(w_in[:depth]),
        "b_gate": np.ascontiguousarray(b_gate[:depth]), "rpbg": rpbg.astype(np.float32),
        "q_norm_g": np.ascontiguousarray(q_norm_g[:depth]), "k_norm_g": np.ascontiguousarray(k_norm_g[:depth]),
        "w_pa": np.ascontiguousarray(w_pa[:depth]), "w_pb": np.ascontiguousarray(w_pb[:depth]),
        "w_out": np.ascontiguousarray(w_out[:depth]), "final_g": np.ascontiguousarray(final_g),
        "rope": _const_tables(nmax), "ident": np.eye(128, dtype=np.float32),
    }


def kernel(x_prompt, x_sample, c_prompt, c_sample, norm_g, w_ada, b_ada, w_in, b_gate, rpb,
           q_norm_g, k_norm_g, w_pa, w_pb, w_out, final_g):
    f = lambda a: np.asarray(a, dtype=np.float32)
    x_prompt, x_sample, c_prompt, c_sample = f(x_prompt), f(x_sample), f(c_prompt), f(c_sample)
    n = 8
    depth = int(np.asarray(w_in).shape[0])
    bp, sp_, _ = x_prompt.shape
    bs, ss, _ = x_sample.shape
    pp, ps_ = bp // n, bs // n
    seq_lens = [sp_] * pp + [ss] * ps_
    nmax = max(seq_lens)
    W = _shared_inputs(f(norm_g), f(w_ada), f(b_ada), f(w_in), f(b_gate), f(rpb), f(q_norm_g), f(k_norm_g),
                       f(w_pa), f(w_pb), f(w_out), f(final_g), depth, nmax)
    nc, _ = build_program(seq_lens, depth)
    in_maps = []
    for c in range(n):
        xs = [x_prompt[c * pp + i] for i in range(pp)] + [x_sample[c * ps_ + i] for i in range(ps_)]
        cs = [c_prompt[c * pp + i] for i in range(pp)] + [c_sample[c * ps_ + i] for i in range(ps_)]
        in_maps.append(_host_inputs(xs, cs, W, depth, nmax))
    res = run_bass_kernel_spmd(nc, in_maps, core_ids=list(range(n)))
    y_prompt = np.empty_like(x_prompt)
    y_sample = np.empty_like(x_sample)
    for c in range(n):
        y = res.results[c]["yout"]
        o = 0
        for i in range(pp):
            y_prompt[c * pp + i] = y[o:o + sp_]
            o += sp_
        for i in range(ps_):
            y_sample[c * ps_ + i] = y[o:o + ss]
            o += ss
    return (y_prompt, y_sample)
```
